# Optimizing a Trainium2 kernel written in Bass

```python
import jax, jax.numpy as jnp
from jax import lax
import numpy as np

D_MODEL = 1024
BATCH = 2
SEQ = 8192
DEPTH = 2

CTX_LEN = 256
GRID_W = 64
N_MIXERS = 2
N_HEADS = 8
N_KV_HEADS = 2
HEAD_DIM = 128
GROUP = N_HEADS // N_KV_HEADS
Q_DIM = N_HEADS * HEAD_DIM
KV_DIM = N_KV_HEADS * HEAD_DIM
QKV_DIM = Q_DIM + 2 * KV_DIM
Q_BLOCK = 128
ROPE_AXIS_DIM = HEAD_DIM // 2
ROPE_THETA = 10000.0
CHUNK = 128
SGU_DIM = 3 * D_MODEL
SGU_GROUPS = 8
SGU_GROUP_DIM = SGU_DIM // SGU_GROUPS
D_FF = 4 * D_MODEL
N_MOD = 6
EPS = 1e-6

kernel_name = "hybrid_gqa_sgu_prefix_dit"


def rms_norm(x, g):
    xf = x.astype(jnp.float32)
    y = xf * lax.rsqrt(jnp.mean(xf * xf, axis=-1, keepdims=True) + EPS) * g.astype(jnp.float32)
    return y.astype(x.dtype)


def modulate(h, shift, scale):
    return h * (1 + scale[:, None, :]) + shift[:, None, :]


def grid_rope_tables(n):
    rows_count = n // GRID_W
    rows = jnp.repeat(jnp.arange(rows_count, dtype=jnp.int32), GRID_W).astype(jnp.float32)
    cols = jnp.tile(jnp.arange(GRID_W, dtype=jnp.int32), rows_count).astype(jnp.float32)
    freqs = 1.0 / (ROPE_THETA ** (jnp.arange(0, ROPE_AXIS_DIM, 2, dtype=jnp.float32) / ROPE_AXIS_DIM))
    ang = jnp.concatenate([rows[:, None] * freqs, cols[:, None] * freqs], axis=-1)
    return jnp.cos(ang), jnp.sin(ang)


def apply_rope(x, cos, sin):
    xf = x.astype(jnp.float32).reshape(x.shape[:-1] + (HEAD_DIM // 2, 2))
    x1, x2 = xf[..., 0], xf[..., 1]
    c = cos[None, :, None, :]
    s = sin[None, :, None, :]
    out = jnp.stack([x1 * c - x2 * s, x1 * s + x2 * c], axis=-1)
    return out.reshape(x.shape).astype(x.dtype)


def block_attention(q, k, v):
    s = jnp.einsum("bqhgd,bkhd->bhgqk", q, k, preferred_element_type=jnp.float32) * (HEAD_DIM ** -0.5)
    p = jax.nn.softmax(s, axis=-1).astype(v.dtype)
    return jnp.einsum("bhgqk,bkhd->bqhgd", p, v)


def gqa_attention(hx, hc, wqkv, q_g, k_g, wo, cos, sin, need_ctx_out):
    b, n, _ = hx.shape
    qkv = hx @ wqkv
    q = qkv[..., :Q_DIM].reshape(b, n, N_HEADS, HEAD_DIM)
    k = qkv[..., Q_DIM:Q_DIM + KV_DIM].reshape(b, n, N_KV_HEADS, HEAD_DIM)
    v = qkv[..., Q_DIM + KV_DIM:].reshape(b, n, N_KV_HEADS, HEAD_DIM)
    q = apply_rope(rms_norm(q, q_g), cos, sin)
    k = apply_rope(rms_norm(k, k_g), cos, sin)
    kvc = hc @ wqkv[:, Q_DIM:]
    kc = rms_norm(kvc[..., :KV_DIM].reshape(b, -1, N_KV_HEADS, HEAD_DIM), k_g)
    vc = kvc[..., KV_DIM:].reshape(b, -1, N_KV_HEADS, HEAD_DIM)
    k_all = jnp.concatenate([kc, k], axis=1)
    v_all = jnp.concatenate([vc, v], axis=1)
    qb = q.reshape(b, n // Q_BLOCK, Q_BLOCK, N_KV_HEADS, GROUP, HEAD_DIM)
    qb = jnp.moveaxis(qb, 1, 0)
    ob = lax.map(lambda qi: block_attention(qi, k_all, v_all), qb)
    ox = jnp.moveaxis(ob, 0, 1).reshape(b, n, Q_DIM)
    yx = ox @ wo
    yc = None
    if need_ctx_out:
        lc = hc.shape[1]
        qc = rms_norm((hc @ wqkv[:, :Q_DIM]).reshape(b, lc, N_HEADS, HEAD_DIM), q_g)
        qc = qc.reshape(b, lc, N_KV_HEADS, GROUP, HEAD_DIM)
        oc = block_attention(qc, kc, vc).reshape(b, lc, Q_DIM)
        yc = oc @ wo
    return yx, yc


def chunked_sgu(h, w_in, b_in, v_g, w_s, b_s, w_out):
    b, n, _ = h.shape
    z = jax.nn.gelu(h @ w_in + b_in, approximate=False)
    u = z[..., :SGU_DIM]
    v = rms_norm(z[..., SGU_DIM:], v_g)
    v = v.reshape(b, n // CHUNK, CHUNK, SGU_GROUPS, SGU_GROUP_DIM)
    sv = jnp.einsum("gpq,bcqgd->bcpgd", w_s, v) + jnp.transpose(b_s)[None, None, :, :, None]
    return (u * sv.reshape(b, n, SGU_DIM)) @ w_out


def sq_relu_mlp(h, w1, w2):
    a = jax.nn.relu(h @ w1)
    return (a * a) @ w2


def setup_inputs(seed: int = 0) -> dict:
    key = jax.random.key(seed)
    ks = jax.random.split(key, 24)
    n_attn = (DEPTH + N_MIXERS - 1) // N_MIXERS
    n_sgu = DEPTH // N_MIXERS
    nrm = jax.random.normal
    f32 = jnp.float32
    return {
        "x": nrm(ks[0], (BATCH, SEQ, D_MODEL), f32),
        "c": nrm(ks[1], (BATCH, D_MODEL), f32),
        "ctx": nrm(ks[2], (BATCH, CTX_LEN, D_MODEL), f32),
        "c_ctx": nrm(ks[3], (D_MODEL,), f32),
        "ada_w": nrm(ks[4], (DEPTH, D_MODEL, N_MOD * D_MODEL), f32) * (0.5 * D_MODEL ** -0.5),
        "ada_b": nrm(ks[5], (DEPTH, N_MOD * D_MODEL), f32) * 0.01,
        "mix_norm_g": 1.0 + 0.02 * nrm(ks[6], (DEPTH, D_MODEL), f32),
        "mlp_norm_g": 1.0 + 0.02 * nrm(ks[7], (DEPTH, D_MODEL), f32),
        "mlp_w1": nrm(ks[8], (DEPTH, D_MODEL, D_FF), f32) * D_MODEL ** -0.5,
        "mlp_w2": nrm(ks[9], (DEPTH, D_FF, D_MODEL), f32) * D_FF ** -0.5,
        "attn_wqkv": nrm(ks[10], (n_attn, D_MODEL, QKV_DIM), f32) * D_MODEL ** -0.5,
        "attn_q_g": 1.0 + 0.02 * nrm(ks[11], (n_attn, HEAD_DIM), f32),
        "attn_k_g": 1.0 + 0.02 * nrm(ks[12], (n_attn, HEAD_DIM), f32),
        "attn_wo": nrm(ks[13], (n_attn, Q_DIM, D_MODEL), f32) * Q_DIM ** -0.5,
        "sgu_w_in": nrm(ks[14], (n_sgu, D_MODEL, 2 * SGU_DIM), f32) * D_MODEL ** -0.5,
        "sgu_b_in": nrm(ks[15], (n_sgu, 2 * SGU_DIM), f32) * 0.01,
        "sgu_v_g": 1.0 + 0.02 * nrm(ks[16], (n_sgu, SGU_DIM), f32),
        "sgu_w_s": nrm(ks[17], (n_sgu, SGU_GROUPS, CHUNK, CHUNK), f32) * CHUNK ** -0.5,
        "sgu_b_s": 1.0 + 0.01 * nrm(ks[18], (n_sgu, SGU_GROUPS, CHUNK), f32),
        "sgu_w_out": nrm(ks[19], (n_sgu, SGU_DIM, D_MODEL), f32) * SGU_DIM ** -0.5,
        "final_g": 1.0 + 0.02 * nrm(ks[20], (D_MODEL,), f32),
    }


def reference(x, c, ctx, c_ctx, ada_w, ada_b, mix_norm_g, mlp_norm_g, mlp_w1, mlp_w2,
              attn_wqkv, attn_q_g, attn_k_g, attn_wo,
              sgu_w_in, sgu_b_in, sgu_v_g, sgu_w_s, sgu_b_s, sgu_w_out, final_g):
    n = x.shape[1]
    cos, sin = grid_rope_tables(n)
    sc = jax.nn.silu(c)
    scc = jax.nn.silu(c_ctx)[None, :]
    for i in range(DEPTH):
        last = i == DEPTH - 1
        use_attn = (i % N_MIXERS) == 0
        j = i // N_MIXERS
        mx = jnp.split(sc @ ada_w[i] + ada_b[i], N_MOD, axis=-1)
        mc = jnp.split(scc @ ada_w[i] + ada_b[i], N_MOD, axis=-1)
        hx = modulate(rms_norm(x, mix_norm_g[i]), mx[0], mx[1])
        hc = None
        if use_attn or not last:
            hc = modulate(rms_norm(ctx, mix_norm_g[i]), mc[0], mc[1])
        if use_attn:
            yx, yc = gqa_attention(hx, hc, attn_wqkv[j], attn_q_g[j], attn_k_g[j], attn_wo[j],
                                   cos, sin, not last)
        else:
            yx = chunked_sgu(hx, sgu_w_in[j], sgu_b_in[j], sgu_v_g[j], sgu_w_s[j], sgu_b_s[j], sgu_w_out[j])
            yc = None
            if not last:
                yc = chunked_sgu(hc, sgu_w_in[j], sgu_b_in[j], sgu_v_g[j], sgu_w_s[j], sgu_b_s[j], sgu_w_out[j])
        x = x + mx[2][:, None, :] * yx
        x = x + mx[5][:, None, :] * sq_relu_mlp(
            modulate(rms_norm(x, mlp_norm_g[i]), mx[3], mx[4]), mlp_w1[i], mlp_w2[i])
        if not last:
            ctx = ctx + mc[2][:, None, :] * yc
            ctx = ctx + mc[5][:, None, :] * sq_relu_mlp(
                modulate(rms_norm(ctx, mlp_norm_g[i]), mc[3], mc[4]), mlp_w1[i], mlp_w2[i])
    return rms_norm(x, final_g)
```

```python
import numpy as np
import concourse.bass as bass
import concourse.mybir as mybir
from concourse.bass_utils import run_bass_kernel_spmd

F32 = mybir.dt.float32
BF16 = mybir.dt.bfloat16
AF = mybir.ActivationFunctionType
ALU = mybir.AluOpType
AX = mybir.AxisListType

D = 1024
KC = 8
T = 2048
SEQ = 8192
CTX = 256
NKT = 66
EPS = 1e-6
SBUF_BASE = 16576
SBUF_CAP = 212736


class Buf:
    __slots__ = ("name", "w", "r", "dsem", "dcnt")

    def __init__(self, name, pending=()):
        self.name = name
        self.w = None
        self.r = list(pending)
        self.dsem = None
        self.dcnt = 0


class Sched:
    ENGS = ("pe", "act", "dve", "pool", "sp")

    def __init__(self, nc):
        self.nc = nc
        self.ops = {e: [] for e in self.ENGS}
        self.sem = {e: nc.alloc_semaphore(name=f"s_{e}") for e in self.ENGS}
        self.cnt = {e: 0 for e in self.ENGS}
        self.waited = {}
        self.pending = {e: False for e in self.ENGS}
        self.dsems = {}

    def _deps(self, eng, reads, writes):
        deps = []
        for b in reads:
            if b.w is not None:
                deps.append(b.w)
        for b in writes:
            if b.w is not None:
                deps.append(b.w)
            deps.extend(b.r)
        waits = []
        for (sem, val, peng) in deps:
            if peng == eng and eng in ("pe", "sp"):
                continue
            key = (eng, id(sem))
            if self.waited.get(key, 0) >= val:
                continue
            self.waited[key] = val
            waits.append((sem, val))
        return waits

    @staticmethod
    def _commit(tok, reads, writes):
        for b in reads:
            b.r.append(tok)
        for b in writes:
            b.w = tok
            b.r = []

    def op(self, eng, fn, reads=(), writes=(), signal=True):
        waits = self._deps(eng, reads, writes)
        if signal:
            self.cnt[eng] += 1
            tok = (self.sem[eng], self.cnt[eng], eng)
            self.pending[eng] = False
        else:
            tok = (self.sem[eng], self.cnt[eng] + 1, eng)
            self.pending[eng] = True
        self.ops[eng].append((waits, fn, (self.sem[eng], 1) if signal else None))
        self._commit(tok, reads, writes)
        return tok

    def dma(self, q, out, in_, reads=(), writes=(), semkey=None):
        slot = writes[0] if writes else reads[0]
        if slot.dsem is None:
            key = semkey if semkey is not None else slot.name
            if key not in self.dsems:
                self.dsems[key] = [self.nc.alloc_semaphore(name=f"d_{key}"), 0]
            slot.dsem = self.dsems[key]
        waits = self._deps(q, reads, writes)
        slot.dsem[1] += 16
        tok = (slot.dsem[0], slot.dsem[1], None)
        self.ops[q].append((waits, lambda e, o=out, i=in_: e.dma_start(out=o, in_=i), (slot.dsem[0], 16)))
        self._commit(tok, reads, writes)
        return tok

    def wait_all(self, eng, bufs):
        waits = self._deps(eng, [], bufs)
        self.ops[eng].append((waits, None, None))

    def emit(self, block):
        nc = self.nc
        for e in self.ENGS:
            assert not self.pending[e], f"engine {e} has unsignalled trailing ops"
        engobj = {"pe": nc.tensor, "act": nc.scalar, "dve": nc.vector, "pool": nc.gpsimd, "sp": nc.sync}

        def run(ename):
            eng = engobj[ename]
            for waits, fn, inc in self.ops[ename]:
                for (sem, val) in waits:
                    eng.wait_ge(sem, val)
                if fn is None:
                    continue
                ins = fn(eng)
                if inc is not None:
                    ins.then_inc(inc[0], inc[1])

        @block.tensor
        def _(e):
            run("pe")

        @block.scalar
        def _(e):
            run("act")

        @block.vector
        def _(e):
            run("dve")

        @block.gpsimd
        def _(e):
            run("pool")

        @block.sync
        def _(e):
            run("sp")


class Arena:
    def __init__(self, nc, base, size, tag):
        self.nc, self.base, self.size, self.tag = nc, base, size, tag
        self.cur = 0
        self.bufs = []
        self.pending = []
        self.n = 0
        self.gen = 0

    def reset(self):
        best = {}
        for tok in self.pending:
            k = id(tok[0])
            if k not in best or best[k][1] < tok[1]:
                best[k] = tok
        for b in self.bufs:
            for tok in ([b.w] if b.w is not None else []) + b.r:
                k = id(tok[0])
                if k not in best or best[k][1] < tok[1]:
                    best[k] = tok
        self.pending = list(best.values())
        self.bufs = []
        self.cur = 0
        self.gen += 1

    def absorb(self, *others):
        best = {id(t[0]): t for t in self.pending}
        for o in others:
            for tok in list(o.pending) + [x for b in o.bufs for x in (([b.w] if b.w is not None else []) + b.r)]:
                k = id(tok[0])
                if k not in best or best[k][1] < tok[1]:
                    best[k] = tok
        self.pending = list(best.values())

    def buf(self, name):
        b = Buf(name, self.pending)
        self.bufs.append(b)
        return b

    def tile(self, name, shape, dtype, nbufs=1):
        esz = 4 if dtype == F32 else 2
        nbytes = int(np.prod(shape[1:])) * esz
        nbytes = (nbytes + 63) // 64 * 64
        off = self.base + self.cur
        self.cur += nbytes
        assert self.cur <= self.size, f"arena {self.tag} overflow at {name}: {self.cur} > {self.size}"
        self.n += 1
        t = self.nc.alloc_sbuf_tensor_at(f"{self.tag}{self.gen}_{name}_{self.n}", list(shape), dtype, offset=off).ap()
        if nbufs == 1:
            return t, self.buf(name)
        return t, [self.buf(f"{name}{i}") for i in range(nbufs)]


def mm(S, out, lhsT, rhs, start, stop, reads, writes, signal, tile_position=None):
    if tile_position is not None:
        return S.op("pe", lambda e: e.matmul(out, lhsT=lhsT, rhs=rhs, start=start, stop=stop, tile_position=tile_position),
                    reads, writes, signal)
    return S.op("pe", lambda e: e.matmul(out, lhsT=lhsT, rhs=rhs, start=start, stop=stop), reads, writes, signal)


def tr(S, out, in_, ident, reads, writes, signal):
    return S.op("pe", lambda e: e.transpose(out=out, in_=in_, identity=ident), reads, writes, signal)


def act(S, out, in_, func, reads, writes, **kw):
    return S.op("act", lambda e: e.activation(out=out, in_=in_, func=func, **kw), reads, writes)


def stt(S, out, in0, scalar, in1, op0, op1, reads, writes):
    return S.op("dve", lambda e: e.scalar_tensor_tensor(out=out, in0=in0, scalar=scalar, in1=in1, op0=op0, op1=op1),
                reads, writes)


def tt(S, eng, out, in0, in1, op, reads, writes):
    return S.op(eng, lambda e: e.tensor_tensor(out=out, in0=in0, in1=in1, op=op), reads, writes)


def ts(S, eng, out, in0, s1, s2, op0, op1, reads, writes):
    if op1 is None:
        return S.op(eng, lambda e: e.tensor_scalar(out=out, in0=in0, scalar1=s1, scalar2=None, op0=op0), reads, writes)
    return S.op(eng, lambda e: e.tensor_scalar(out=out, in0=in0, scalar1=s1, scalar2=s2, op0=op0, op1=op1), reads, writes)


def cp(S, eng, out, in_, reads, writes):
    if eng == "act":
        return S.op("act", lambda e: e.copy(out=out, in_=in_), reads, writes)
    return S.op(eng, lambda e: e.tensor_copy(out=out, in_=in_), reads, writes)


def recip(S, out, in_, reads, writes):
    return S.op("dve", lambda e: e.reciprocal(out=out, in_=in_), reads, writes)


V_ADAB0, V_ADAB1, V_MIXG0, V_MIXG1, V_MLPG0, V_MLPG1, V_BINU, V_VG = 0, 48, 96, 104, 112, 120, 128, 152
NVEC = 176


def build(stage=99):
    nc = bass.Bass("TRN2", target_bir_lowering=False)

    def din(name, shape):
        return nc.dram_tensor(name, list(shape), F32, kind="ExternalInput").ap()

    xs = din("xs", [SEQ, D])
    ctx = din("ctx", [CTX, D])
    cs_d = din("cs", [NKT * 128, 64])
    sn_d = din("sn", [NKT * 128, 64])
    cvec = din("cvec", [2, D])
    ada_w = din("ada_w", [2, D, 6 * D])
    vecs_d = din("vecs", [128, NVEC])
    mixg0_d = din("mixg0", [D])
    finalg_d = din("final_g", [D])
    qg_d = din("q_g", [128])
    kg_d = din("k_g", [128])
    binv_d = din("binv", [1, 3072])
    bs_d = din("b_s", [1024])
    wqkv_d = din("wqkv", [D, 1536])
    wo_d = din("wo", [D, D])
    w1_d = din("w1", [2, D, 4 * D])
    w2_d = din("w2", [2, 4 * D, D])
    win_d = din("w_in", [D, 6144])
    wout_d = din("w_out", [3072, D])
    wsT_d = din("wsT", [128, 8, 128])
    ident_d = din("ident", [128, 128])
    out_d = nc.dram_tensor("out", [T, D], F32, kind="ExternalOutput").ap()

    S = Sched(nc)
    SM = Arena(nc, SBUF_BASE, 3584, "sm")
    modT = []
    for l in range(2):
        modT.append(SM.tile(f"modT{l}", [128, 48, 2], F32))
    vecs, vecs_b = SM.tile("vecs", [128, NVEC], F32)
    identf, identf_b = SM.tile("identf", [128, 128], F32)
    identb, identb_b = SM.tile("identb", [128, 128], BF16)
    onesf, onesf_b = SM.tile("onesf", [128, 128], F32)
    onesb, onesb_b = SM.tile("onesb", [128, 128], BF16)
    scT, scT_b = SM.tile("scT", [128, 8, 2], F32)
    scTb, scTb_b = SM.tile("scTb", [128, 8, 2], BF16)
    cT, cT_b = SM.tile("cT", [128, 2, 8], F32)
    gms = {}
    for nm in ("gm_mlp0", "gm_mix1", "gm_mlp1", "gm0x", "gm0c"):
        gms[nm] = SM.tile(nm, [128, 8], F32)
    SM_SZ = 3584
    FREE = SBUF_CAP - SM_SZ
    TOP_SZ = 2 * 33792 + 32768
    BOT = Arena(nc, SBUF_BASE + SM_SZ, FREE - TOP_SZ, "bot")
    TOP = Arena(nc, SBUF_BASE + SM_SZ + FREE - TOP_SZ, TOP_SZ, "top")
    A = Arena(nc, SBUF_BASE + SM_SZ + 65536, FREE - 65536, "hi")

    ps = nc.alloc_psum_tensor("ps", [128, 8, 512], F32).ap()
    bank = [Buf(f"bank{i}") for i in range(8)]

    def psb(i):
        return ps[:, i, :]

    def psb16(i):
        return ps[:, i, :].bitcast(BF16)

    with nc.Block() as block:
        S.dma("sp", identf, ident_d, writes=[identf_b])
        S.dma("sp", vecs, vecs_d, writes=[vecs_b])
        S.dma("sp", cT, cvec.rearrange("j (p k) -> p j k", k=8), writes=[cT_b])
        cp(S, "dve", identb, identf, [identf_b], [identb_b])
        S.op("pool", lambda e: e.memset(onesf, 1.0), [], [onesf_b])
        S.op("pool", lambda e: e.memset(onesb, 1.0), [], [onesb_b])
        act(S, scT.rearrange("p k j -> p j k"), cT, AF.Silu, [cT_b], [scT_b])
        cp(S, "dve", scTb, scT, [scT_b], [scTb_b])

        def mods_gen(l, slots, chunk_cols, bks, col0, col1, tag, prime=False):
            mt, mt_b = modT[l]
            nch = (col1 - col0) // chunk_cols
            src = ada_w[l].rearrange("(p k) n -> p k n", k=8)
            jper = chunk_cols // 128
            off = V_ADAB0 if l == 0 else V_ADAB1

            def load(ci):
                buf, bb = slots[ci % 2]
                S.dma("pool", buf[:, :, 0:chunk_cols], src[:, :, col0 + ci * chunk_cols:col0 + (ci + 1) * chunk_cols],
                      writes=[bb], semkey=f"ada{tag}_{ci % 2}")

            load(0)
            if prime:
                yield
            for ci in range(nch):
                if ci + 1 < nch:
                    load(ci + 1)
                buf, bb = slots[ci % 2]
                bk = bks[ci % len(bks)]
                j0 = col0 // 128 + ci * jper
                for jj in range(jper):
                    for k in range(8):
                        mm(S, ps[:, bk, 2 * jj:2 * jj + 2], buf[:, k, jj * 128:(jj + 1) * 128], scTb[:, k, :], k == 0, k == 7,
                           [bb, scTb_b], [bank[bk]], jj == jper - 1 and k == 7)
                tt(S, "dve", mt[:, j0:j0 + jper, :], ps[:, bk, 0:2 * jper].rearrange("p (j c) -> p j c", c=2),
                   vecs[:, off + j0:off + j0 + jper].unsqueeze(2).broadcast_to([128, jper, 2]), ALU.add,
                   [bank[bk], vecs_b], [mt_b])
                yield

        ada_slots0 = [BOT.tile(f"ada0_{i}", [128, 8, 512], BF16) for i in range(2)]
        for _ in mods_gen(0, ada_slots0, 512, [6, 7], 0, 2048, "s"):
            pass

        def gm_make(nm, gcol, l, m):
            g, g_b = gms[nm]
            mt, mt_b = modT[l]
            stt(S, g, mt[:, m * 8:(m + 1) * 8, 0], 1.0, vecs[:, gcol:gcol + 8], ALU.add, ALU.mult, [mt_b, vecs_b], [g_b])


        def mod_s(l, m, j):
            return modT[l][0][:, m * 8 + j, 0:1]

        KT, KT_b = TOP.tile("KT", [128, 2, NKT * 128], BF16)
        Vs, Vs_b = TOP.tile("Vs", [128, NKT, 256], BF16)
        QT, _qtb = TOP.tile("QT", [128, 8, T], BF16)
        QT_b = [[TOP.buf(f"QT{q}_{h}") for h in range(8)] for q in range(4)]
        BOT.reset()
        wq, wq_b = BOT.tile("wq", [128, 8, 1024], BF16)
        wkv, wkv_b = BOT.tile("wkv", [128, 8, 512], BF16)
        S.dma("pool", wkv, wqkv_d[:, 1024:1536].rearrange("(k p) n -> p k n", p=128), writes=[wkv_b])
        S.dma("pool", wq, wqkv_d[:, 0:1024].rearrange("(k p) n -> p k n", p=128), writes=[wq_b])
        wkvc, wkvc_b = BOT.tile("wkvc", [128, 8, 512], BF16)
        bkv_bc = [BOT.tile(f"bkv_bc{c}", [128, 512], F32) for c in range(2)]
        bq_bc, bq_b = BOT.tile("bq_bc", [128, 1024], F32)
        kqg, kqg_b = BOT.tile("kqg", [128, 10, 128], F32)
        for hh in range(10):
            S.dma("sp", kqg[:, hh, :], (kg_d if hh < 2 else qg_d).partition_broadcast(128), writes=[kqg_b], semkey="kqg")
        prep_keep_n = len(BOT.bufs)
        prep_keep_cur = BOT.cur
        LO = BOT

        shiftB, shiftB_b = LO.tile("shiftB", [128, 8, 128], BF16)
        src2 = ps[:, 0:2, :].rearrange("p a b -> p (a b)")
        for col in range(2):
            for j in range(8):
                S.op("dve", lambda e, j=j, col=col: e.tensor_scalar(
                    out=shiftB[:, j, :], in0=onesb, scalar1=modT[0][0][:, j, col:col + 1], scalar2=None, op0=ALU.mult),
                    [onesb_b, modT[0][1]], [shiftB_b])
            for j in range(8):
                mm(S, psb(2), shiftB[:, j, :], wkv[:, j, :], j == 0, j == 7, [shiftB_b, wkv_b], [bank[2]], j == 7)
            cp(S, "act", bkv_bc[col][0], psb(2), [bank[2]], [bkv_bc[col][1]])
            if col == 0:
                for half in range(2):
                    for j in range(8):
                        mm(S, psb(half), shiftB[:, j, :], wq[:, j, half * 512:(half + 1) * 512], j == 0, j == 7,
                           [shiftB_b, wq_b], [bank[half]], j == 7)
                cp(S, "act", bq_bc, src2, [bank[0], bank[1]], [bq_b])
        for col, nm in ((0, "gm0x"), (1, "gm0c")):
            g, g_b = gms[nm]
            stt(S, g, modT[0][0][:, 8:16, col], 1.0, vecs[:, V_MIXG0:V_MIXG0 + 8], ALU.add, ALU.mult, [modT[0][1], vecs_b], [g_b])
        for k in range(8):
            ts(S, "dve", wkvc[:, k, :], wkv[:, k, :], gms["gm0c"][0][:, k:k + 1], None, ALU.mult, None,
               [wkv_b, gms["gm0c"][1]], [wkvc_b])
        for k in range(8):
            ts(S, "dve", wkv[:, k, :], wkv[:, k, :], gms["gm0x"][0][:, k:k + 1], None, ALU.mult, None,
               [wkv_b, gms["gm0x"][1]], [wkv_b])
            ts(S, "dve", wq[:, k, :], wq[:, k, :], gms["gm0x"][0][:, k:k + 1], None, ALU.mult, None,
               [wq_b, gms["gm0x"][1]], [wq_b])

        keep = BOT.bufs[:prep_keep_n]
        BOT.reset()
        BOT.bufs = keep
        BOT.cur = prep_keep_cur

        def ring(ar, name, shape, dtype, n):
            ap, bufs = ar.tile(name, [shape[0], n] + list(shape[1:]), dtype, nbufs=n)
            if n == 1:
                bufs = [bufs]
            return [ap[:, i] for i in range(n)], bufs

        NX = 3
        NTAB = 6
        xt, xt_b = ring(LO, "xt", [128, D], F32, NX)
        ctab, ctab_b = ring(LO, "ctab", [128, 64], F32, NTAB)
        stab, stab_b = ring(LO, "stab", [128, 64], F32, NTAB)
        junk, junk_b = LO.tile("junk", [128, D], BF16)
        NXB = 3
        xb, xb_b = ring(LO, "xb", [128, D], BF16, NXB)
        hTs, hTs_b = ring(LO, "hTs", [128, 8, 128], BF16, 2)
        kq32, kq32_b = ring(LO, "kq32", [128, 10, 128], F32, 2)
        sqj, sqj_b = ring(LO, "sqj", [128, 10, 128], BF16, 1)
        rr32, rr32_b = ring(LO, "rr32", [128, 10, 128], F32, 2)
        rpb, rpb_b = LO.tile("rpb", [128, 10, 64], F32)
        rdb, rdb_b = LO.tile("rdb", [128, 10, 64], F32)
        rr32e_b = [LO.buf(f"rr32e{i}") for i in range(2)]
        rr32o_b = [LO.buf(f"rr32o{i}") for i in range(2)]
        krq, krq_b = ring(LO, "krq", [128, 10, 128], BF16, 1)
        ms, ms_b = ring(LO, "ms", [128, 4], F32, 4)
        ss, ss_b = ring(LO, "ss", [128, 32], F32, 3)

        def prep_load(t):
            s = t % NX
            s6 = t % NTAB
            src = xs[t * 128:(t + 1) * 128, :] if t < 64 else ctx[(t - 64) * 128:(t - 63) * 128, :]
            S.dma("sp", xt[s], src, writes=[xt_b[s]], semkey=f"xt{s}")
            S.dma("pool", xb[t % NXB], src, writes=[xb_b[t % NXB]], semkey=f"xb{t % NXB}")
            S.dma("sp", ctab[s6], cs_d[t * 128:(t + 1) * 128, :], writes=[ctab_b[s6]], semkey=f"ct{s6}")
            S.dma("sp", stab[s6], sn_d[t * 128:(t + 1) * 128, :], writes=[stab_b[s6]], semkey=f"st{s6}")

        def st1(t):
            s, col, m, mb = t % NX, (0 if t < 64 else 1), ms[t % 4], ms_b[t % 4]
            act(S, junk, xt[s], AF.Square, [xt_b[s]], [junk_b, mb], scale=1.0 / 32, accum_out=m[:, 0:1])
            act(S, m[:, 1:2], m[:, 0:1], AF.Sqrt, [mb], [mb], bias=EPS, scale=1.0)

        def st2_pe(t):
            p = t % 2
            hb = 4 + p
            for j in range(8):
                tr(S, psb16(hb)[:, j * 128:(j + 1) * 128], xb[t % NXB][:, j * 128:(j + 1) * 128], identb,
                   [xb_b[t % NXB], identb_b], [bank[hb]], j == 7)

        def st2_dve(t):
            m, mb = ms[t % 4], ms_b[t % 4]
            recip(S, m[:, 2:3], m[:, 1:2], [mb], [mb])

        def st2_act(t):
            p = t % 2
            hb = 4 + p
            cp(S, "act", hTs[p], psb16(hb).rearrange("p (a b) -> p a b", b=128), [bank[hb]], [hTs_b[p]])

        def st3_pe(t):
            p = t % 2
            bk = 2 + p
            wk_, wkb_ = (wkv, wkv_b) if t < 64 else (wkvc, wkvc_b)
            for k in range(8):
                mm(S, psb(bk), hTs[p][:, k, :], wk_[:, k, :], k == 0, k == 7, [hTs_b[p], wkb_], [bank[bk]], k == 7)
            if t < 16:
                for half in range(2):
                    for k in range(8):
                        mm(S, psb(half), hTs[p][:, k, :], wq[:, k, half * 512:(half + 1) * 512], k == 0, k == 7,
                           [hTs_b[p], wq_b], [bank[half]], k == 7)

        def st3_dve(t):
            m, mb, p, col = ms[t % 4], ms_b[t % 4], t % 2, (0 if t < 64 else 1)
            bk = 2 + p
            kq = kq32[p]
            bkv, bkvb = bkv_bc[col]
            stt(S, kq[:, 0:2, :], ps[:, bk, 0:256].rearrange("p (h d) -> p h d", d=128), m[:, 2:3],
                bkv[:, 0:256].rearrange("p (h d) -> p h d", d=128), ALU.mult, ALU.add, [bank[bk], mb, bkvb], [kq32_b[p]])
            stt(S, Vs[:, t, :], ps[:, bk, 256:512], m[:, 2:3], bkv[:, 256:512], ALU.mult, ALU.add, [bank[bk], mb, bkvb], [Vs_b])
            if t < 16:
                stt(S, kq[:, 2:10, :], ps[:, 0:2, :].rearrange("p a (b c) -> p (a b) c", c=128), m[:, 2:3],
                    bq_bc.rearrange("p (h d) -> p h d", d=128), ALU.mult, ALU.add, [bank[0], bank[1], mb, bq_b], [kq32_b[p]])

        def st3_act(t):
            p = t % 2
            nh = 10 if t < 16 else 2
            kq = kq32[p]
            act(S, sqj[0][:, 0:nh, :], kq[:, 0:nh, :], AF.Square, [kq32_b[p]], [sqj_b[0]])
            tt(S, "dve" if t < 16 else "pool", kq[:, 0:nh, :], kq[:, 0:nh, :], kqg[:, 0:nh, :], ALU.mult,
               [kq32_b[p], kqg_b, sqj_b[0]], [kq32_b[p]])

        def st4(t):
            p, own, s6 = t % 2, t < 16, t % NTAB
            nh = 10 if own else 2
            sv, svb = ss[t % 3], ss_b[t % 3]
            S.op("dve", lambda e: e.tensor_reduce(out=sv[:, 0:nh], in_=sqj[0][:, 0:nh, :], axis=AX.X, op=ALU.add),
                 [sqj_b[0]], [svb])
            act(S, sv[:, 16:16 + nh], sv[:, 0:nh], AF.Sqrt, [svb], [svb], bias=EPS, scale=1.0 / 128)
            src = kq32[p][:, 0:nh, :]
            dst = rr32[p][:, 0:nh, :]
            x1, x2 = src[:, :, 0::2], src[:, :, 1::2]
            cb = ctab[s6].unsqueeze(1).broadcast_to([128, nh, 64])
            sb = stab[s6].unsqueeze(1).broadcast_to([128, nh, 64])
            rd = [kq32_b[p], ctab_b[s6], stab_b[s6]]
            b_ = rpb[:, 0:nh, :]
            tt(S, "pool", dst[:, :, 0::2], x1, cb, ALU.mult, rd, [rr32e_b[p]])
            tt(S, "pool", b_, x2, sb, ALU.mult, rd, [rpb_b])
            tt(S, "pool", dst[:, :, 0::2], dst[:, :, 0::2], b_, ALU.subtract, [rpb_b, rr32e_b[p]], [rr32e_b[p]])
            b_ = rdb[:, 0:nh, :]
            tt(S, "dve", dst[:, :, 1::2], x1, sb, ALU.mult, rd, [rr32o_b[p]])
            tt(S, "dve", b_, x2, cb, ALU.mult, rd, [rdb_b])
            tt(S, "dve", dst[:, :, 1::2], dst[:, :, 1::2], b_, ALU.add, [rdb_b, rr32o_b[p]], [rr32o_b[p]])

        def st5_dve(t):
            p = t % 2
            nh = 10 if t < 16 else 2
            sv, svb = ss[t % 3], ss_b[t % 3]
            recip(S, sv[:, 16:16 + nh], sv[:, 16:16 + nh], [svb], [svb])
            tt(S, "dve", krq[0][:, 0:nh, :], rr32[p][:, 0:nh, :], sv[:, 16:16 + nh].unsqueeze(2).broadcast_to([128, nh, 128]),
               ALU.mult, [rr32e_b[p], rr32o_b[p], svb], [krq_b[0]])

        def st5_pe(t):
            for h in range(2):
                tr(S, psb16(7)[:, h * 128:(h + 1) * 128], krq[0][:, h, :], identb, [krq_b[0], identb_b], [bank[7]], h == 1)
            if t < 16:
                for h in range(8):
                    tr(S, psb16(6)[:, h * 128:(h + 1) * 128], krq[0][:, 2 + h, :], identb, [krq_b[0], identb_b], [bank[6]], h == 7)

        def st5_act(t):
            cp(S, "act", KT[:, :, t * 128:(t + 1) * 128], psb16(7)[:, 0:256].rearrange("p (h d) -> p h d", d=128),
               [bank[7]], [KT_b])
            if t < 16:
                cp(S, "act", QT[:, :, t * 128:(t + 1) * 128], psb16(6).rearrange("p (h d) -> p h d", d=128),
                   [bank[6]], QT_b[t // 4])

        def ok(t):
            return 0 <= t < NKT

        prep_load(0)
        prep_load(1)
        for it in range(NKT + 4):
            t1, t2, t3, t4, t5 = it, it - 1, it - 2, it - 3, it - 4
            if ok(t5):
                st5_dve(t5)
            if ok(t4):
                st4(t4)
            if ok(t1):
                st1(t1)
            if ok(t3):
                st3_pe(t3)
            if ok(t2):
                st2_pe(t2)
            if ok(t5):
                st5_pe(t5)
            if ok(t3):
                st3_dve(t3)
            if ok(t2):
                st2_dve(t2)
                st2_act(t2)
            if ok(t5):
                st5_act(t5)
            if ok(t3):
                st3_act(t3)
            if it + 2 < NKT:
                prep_load(it + 2)

        def fm_stat(tok0, ntok, sq_t, rs_t, xbufs, sbank=7):
            sq, sq_b = sq_t
            rs, rs_b = rs_t
            xv = xT[:, :, tok0:tok0 + ntok]
            act(S, sq[:, :, 0:ntok], xv, AF.Square, xbufs, [sq_b])
            for j in range(8):
                mm(S, ps[:, sbank, 0:ntok], onesb, sq[:, j, 0:ntok], j == 0, j == 7, [onesb_b, sq_b], [bank[sbank]], j == 7)
            act(S, rs[:, 0:ntok], ps[:, sbank, 0:ntok], AF.Sqrt, [bank[sbank]], [rs_b], bias=EPS, scale=1.0 / D)
            recip(S, rs[:, 0:ntok], rs[:, 0:ntok], [rs_b], [rs_b])

        def fm_mod(tok0, ntok, gm_ap, l, m_shift, hT_out, hT_out_b, rs_t, tmp_t, xbufs, gm_b=None):
            rs, rs_b = rs_t
            tmp, tmp_b = tmp_t
            for j in range(8):
                i2 = j % 2
                stt(S, tmp[:, i2, 0:ntok], xT[:, j, tok0:tok0 + ntok], gm_ap[:, j:j + 1], rs[:, 0:ntok], ALU.mult, ALU.mult,
                    xbufs + [rs_b] + ([gm_b] if gm_b is not None else []), [tmp_b[i2]])
                act(S, hT_out[:, j, :], tmp[:, i2, 0:ntok], AF.Identity, [tmp_b[i2], modT[l][1]], [hT_out_b],
                    bias=mod_s(l, m_shift, j), scale=1.0)

        def fm_norm(ar, tok0, ntok, gm_ap, l, m_shift, hT_out, hT_out_b, sq_t, rs_t, tmp_t, xbufs, gm_b=None):
            fm_stat(tok0, ntok, sq_t, rs_t, xbufs)
            fm_mod(tok0, ntok, gm_ap, l, m_shift, hT_out, hT_out_b, rs_t, tmp_t, xbufs, gm_b)

        if stage >= 2:
            BOT.reset()
            xT, _x = BOT.tile("xT", [128, KC, T], F32)
            xTb = [BOT.buf(f"xT{i}") for i in range(4)]
            wo, wo_b = BOT.tile("wo", [128, 8, 1024], BF16)
            S.dma("pool", wo, wo_d.rearrange("(h p) n -> p h n", p=128), writes=[wo_b])
            NPT = 8
            G = 6
            PT, PT_b = BOT.tile("PT", [128, NPT, 512], BF16, nbufs=NPT)
            rcp, rcp_b = BOT.tile("rcp", [128, 512], F32)
            Rsb, Rsb_b = BOT.tile("Rsb", [128, 2, 512], F32, nbufs=2)
            xr, xr_b = BOT.tile("xr", [128, 1, D], F32, nbufs=2)
            xr_b = [xr_b, xr_b] if not isinstance(xr_b, list) else xr_b
            ada_slots1 = [BOT.tile(f"ada1_{i}", [128, 8, 128], BF16) for i in range(2)]
            acc, acc_b = BOT.tile("acc", [128, 2, 512], F32, nbufs=2)
            NDVE = 3

            def _chain():
                yield from mods_gen(0, ada_slots1, 128, [7], 2048, 6144, "a", prime=True)
            mods1 = _chain()
            next(mods1, None)
            SCALE = 128.0 ** -0.5
            step = 0
            nxr = 0
            nunit = 0
            pend = []

            def epi_a(u, rbk):
                ts(S, "dve", Rsb[0:96, u % 2, :], ps[0:96, rbk, :], 1.0 / 32, None, ALU.mult, None, [bank[rbk]], [Rsb_b[u % 2]])

            def epi_b(u, ob, qv, qbuf):
                mm(S, psb(7), onesf, acc[:, u % 2, :], True, False, [onesf_b, acc_b[u % 2]], [bank[7]], False)
                mm(S, psb(7), onesf[0:96, :], Rsb[0:96, u % 2, :], False, True, [onesf_b, Rsb_b[u % 2]], [bank[7]], True)
                recip(S, rcp, psb(7), [bank[7]], [rcp_b])
                tt(S, "dve", qv, psb(ob), rcp, ALU.mult, [bank[ob], rcp_b], [qbuf])

            for qb in range(4):
                tok0 = qb * 512
                def make_xT_block(qb=qb):
                    for i in range(4):
                        tg = qb * 4 + i
                        sl = 0
                        S.dma("sp", xr[:, sl, :], xs[tg * 128:(tg + 1) * 128, :], writes=[xr_b[sl]], semkey=f"xr{sl}")
                        for half in range(2):
                            for jj in range(4):
                                j = half * 4 + jj
                                tr(S, ps[:, 7, jj * 128:(jj + 1) * 128], xr[:, sl, j * 128:(j + 1) * 128], identf,
                                   [xr_b[sl], identf_b], [bank[7]], jj == 3)
                            cp(S, "dve", xT[:, half * 4:(half + 1) * 4, tg * 128:(tg + 1) * 128],
                               ps[:, 7, :].rearrange("p (b c) -> p b c", c=128), [bank[7]], [xTb[qb]])
                            yield

                xgen = make_xT_block()

                for h in range(8):
                    kvh = h // 4
                    ob = 4 + (h % 2)
                    rbk = 6
                    qv = QT[:, h, tok0:tok0 + 512]
                    qbuf = QT_b[qb][h]

                    def s_mm(kt):
                        sb_ = (step + kt) % 4
                        mm(S, psb(sb_), KT[:, kvh, kt * 128:(kt + 1) * 128], qv, True, True,
                           [KT_b, qbuf], [bank[sb_]], True)

                    s_mm(0)
                    s_mm(1)
                    s_mm(2)
                    for g0 in range(0, NKT, G):
                        for kt in range(g0, g0 + G):
                            sb_ = (step + kt) % 4
                            pb = (step + kt) % NPT
                            act(S, PT[:, pb, :], psb(sb_), AF.Exp, [bank[sb_]], [PT_b[pb]], scale=SCALE)
                            if kt + 3 < NKT:
                                s_mm(kt + 3)
                            mm(S, psb(ob), Vs[:, kt, kvh * 128:(kvh + 1) * 128], PT[:, pb, :], kt == 0, kt == NKT - 1,
                               [Vs_b, PT_b[pb]], [bank[ob]], kt == NKT - 1)
                        ndve = 0 if g0 == NKT - G else NDVE
                        for i in range(G):
                            kt = g0 + i
                            pb = (step + kt) % NPT
                            if i < ndve:
                                a_ = acc[:, nunit % 2, :]
                                if g0 == 0 and i == 0:
                                    cp(S, "dve", a_, PT[:, pb, :], [PT_b[pb]], [acc_b[nunit % 2]])
                                else:
                                    tt(S, "dve", a_, a_, PT[:, pb, :], ALU.add, [PT_b[pb], acc_b[nunit % 2]], [acc_b[nunit % 2]])
                            else:
                                cg = (i - ndve) % 3
                                last_use = (g0 == NKT - G) and (i - ndve) >= (G - ndve) - 3
                                mm(S, ps[32 * cg:32 * cg + 32, rbk, :], onesb[:, 0:32], PT[:, pb, :], g0 == 0, last_use,
                                   [onesb_b, PT_b[pb]], [bank[rbk]], i == G - 1, tile_position=(0, 32 * cg))
                        if g0 == 2 * G and pend:
                            epi_b(*pend.pop(0))
                        if g0 in (3 * G, 5 * G, 7 * G, 9 * G):
                            next(mods1, None)
                    step += NKT
                    next(xgen, None)
                    epi_a(nunit, rbk)
                    pend.append((nunit, ob, qv, qbuf))
                    nunit += 1
                for _ in xgen:
                    pass
                while pend:
                    epi_b(*pend.pop(0))
                for j in range(8):
                    wb = (7, 6, 4, 5)[j % 4]
                    for h in range(8):
                        mm(S, psb(wb), wo[:, h, j * 128:(j + 1) * 128], QT[:, h, tok0:tok0 + 512], h == 0, h == 7,
                           [wo_b, QT_b[qb][h]], [bank[wb]], h == 7)
                    stt(S, xT[:, j, tok0:tok0 + 512], psb(wb), mod_s(0, 2, j), xT[:, j, tok0:tok0 + 512], ALU.mult, ALU.add,
                        [bank[wb], modT[0][1], xTb[qb]], [xTb[qb]])

        if stage < 2:
            BOT.reset()
            xT, _x = BOT.tile("xT", [128, KC, T], F32)
            xTb = [BOT.buf(f"xT{i}") for i in range(4)]
            xr, xr_b = BOT.tile("xr", [128, 2, D], F32, nbufs=2)
            for tg in range(16):
                sl = tg % 2
                S.dma("sp", xr[:, sl, :], xs[tg * 128:(tg + 1) * 128, :], writes=[xr_b[sl]], semkey=f"xr{sl}")
                for half in range(2):
                    for jj in range(4):
                        j = half * 4 + jj
                        tr(S, ps[:, 7, jj * 128:(jj + 1) * 128], xr[:, sl, j * 128:(j + 1) * 128], identf,
                           [xr_b[sl], identf_b], [bank[7]], jj == 3)
                    cp(S, "dve", xT[:, half * 4:(half + 1) * 4, tg * 128:(tg + 1) * 128],
                       ps[:, 7, :].rearrange("p (b c) -> p b c", c=128), [bank[7]], [xTb[tg // 4]])
        if stage >= 2:
            for _ in mods1:
                pass
            gm_make("gm_mlp0", V_MLPG0, 0, 4)
        A.absorb(BOT, TOP)

        def mlp(l):
            A.reset()
            hT, hT_b = A.tile("hT", [128, 8, T], BF16, nbufs=4)
            sq_t = A.tile("sq", [128, 8, 512], BF16)
            rs_t = A.tile("rs", [128, 512], F32)
            tmp_t = A.tile("tmp", [128, 2, 512], F32, nbufs=2)
            NW = 3
            w1s, w1s_b = A.tile("w1s", [128, NW, 8, 512], BF16, nbufs=NW)
            w2s, w2s_b = A.tile("w2s", [128, NW, 4, 1024], BF16, nbufs=NW)
            rr, rr_b = A.tile("rr", [128, 2, 512], F32, nbufs=2)
            aT, aT_b = A.tile("aT", [128, 8, 512], BF16, nbufs=8)
            gm, gm_buf = gms["gm_mlp0" if l == 0 else "gm_mlp1"]

            def load_w(f):
                sl = f % NW
                S.dma("pool", w1s[:, sl], w1_d[l][:, f * 512:(f + 1) * 512].rearrange("(k p) n -> p k n", p=128),
                      writes=[w1s_b[sl]], semkey=f"w1s{sl}")
                S.dma("pool", w2s[:, sl], w2_d[l][f * 512:(f + 1) * 512, :].rearrange("(c p) n -> p c n", p=128),
                      writes=[w2s_b[sl]], semkey=f"w2s{sl}")

            load_w(0)
            load_w(1)
            load_w(2)
            sq_t2 = A.tile("sq2", [128, 8, 512], BF16)
            rs_t2 = A.tile("rs2", [128, 512], F32)
            sqs, rss = [sq_t, sq_t2], [rs_t, rs_t2]

            def n_stat(tb):
                fm_stat(tb * 512, 512, sqs[tb % 2], rss[tb % 2], [xTb[tb]], sbank=6 + tb % 2)

            def n_mod(tb):
                fm_mod(tb * 512, 512, gm, l, 3, hT[:, :, tb * 512:(tb + 1) * 512], hT_b[tb], rss[tb % 2], tmp_t, [xTb[tb]], gm_buf)

            n_stat(0)
            n_stat(1)
            n_mod(0)
            hooks = {0: [lambda: n_mod(1), lambda: n_stat(2)], 1: [lambda: n_mod(2), lambda: n_stat(3)], 2: [lambda: n_mod(3)]}
            units = [(f, tb) for tb in range(4) for f in range(3)] + [(f, tb) for f in range(3, 8) for tb in range(4)]
            last_use = {}
            for n_, (f_, tb_) in enumerate(units):
                last_use[f_] = n_
            asets = {}
            cnt = {"na": 0, "ny": 0}

            def a_part(n):
                f, tb = units[n]
                sl = f % NW
                aset = []
                for c in range(4):
                    na = cnt["na"]
                    cnt["na"] += 1
                    bk, r2, a8 = na % 2, na % 2, na % 8
                    for k in range(8):
                        mm(S, psb(bk), w1s[:, sl, k, c * 128:(c + 1) * 128], hT[:, k, tb * 512:(tb + 1) * 512],
                           k == 0, k == 7, [w1s_b[sl], hT_b[tb]], [bank[bk]], k == 7)
                    act(S, rr[:, r2, :], psb(bk), AF.Relu, [bank[bk]], [rr_b[r2]])
                    tt(S, "pool", aT[:, a8, :], rr[:, r2, :], rr[:, r2, :], ALU.mult, [rr_b[r2]], [aT_b[a8]])
                    aset.append(a8)
                asets[n] = aset

            def y_part(n):
                f, tb = units[n]
                sl = f % NW
                aset = asets.pop(n)
                for j in range(8):
                    bk = 2 + (cnt["ny"] % 3)
                    cnt["ny"] += 1
                    for c in range(4):
                        mm(S, psb(bk), w2s[:, sl, c, j * 128:(j + 1) * 128], aT[:, aset[c], :], c == 0, c == 3,
                           [w2s_b[sl], aT_b[aset[c]]], [bank[bk]], c == 3)
                    xv = xT[:, j, tb * 512:(tb + 1) * 512]
                    stt(S, xv, psb(bk), mod_s(l, 5, j), xv, ALU.mult, ALU.add, [bank[bk], modT[l][1], xTb[tb]], [xTb[tb]])
                if last_use[f] == n and f + 3 < 8:
                    load_w(f + 3)

            bg = None
            if l == 0:
                ada_slots2 = [A.tile(f"ada2_{i}", [128, 8, 512], BF16) for i in range(2)]
                bg = mods_gen(1, ada_slots2, 512, [5], 0, 6144, "m", prime=True)
                next(bg, None)
            a_part(0)
            for n in range(len(units)):
                for fn in hooks.get(n, []):
                    fn()
                if n + 1 < len(units):
                    a_part(n + 1)
                y_part(n)
                if bg is not None and n >= 4:
                    next(bg, None)
            if bg is not None:
                for _ in bg:
                    pass

        if stage >= 3:
            mlp(0)

        if stage >= 4:
            gm_make("gm_mix1", V_MIXG1, 1, 1)
            gm_make("gm_mlp1", V_MLPG1, 1, 4)
            A.reset()
            hT1, hT1_b = A.tile("hT1", [128, 8, 1024], BF16, nbufs=2)
            vraw, vraw_b = A.tile("vraw", [128, 8, 3072], BF16, nbufs=8)
            sq_t = A.tile("sq", [128, 8, 512], BF16)
            rs_t = A.tile("rs", [128, 512], F32)
            tmp_t = A.tile("tmp", [128, 2, 512], F32, nbufs=2)
            wins, wins_b = A.tile("wins", [128, 2, 8, 512], BF16, nbufs=2)
            bvs, bvs_b = A.tile("bvs", [1, 2, 512], BF16, nbufs=2)
            wouts, wouts_b = A.tile("wouts", [128, 2, 3, 1024], BF16, nbufs=2)
            wsTb, wsTb_b = A.tile("wsTb", [128, 8, 128], BF16)
            wsr, wsr_b = A.tile("wsr", [128, 2, 8, 128], BF16, nbufs=2)
            bs_bc, bs_b = A.tile("bs_bc", [128, 8, 128], F32)
            uT, uT_b = A.tile("uT", [128, 6, 512], BF16, nbufs=6)
            mT, mT_b = A.tile("mT", [128, 6, 512], BF16, nbufs=6)
            svt, svt_b = A.tile("svt", [128, 2, 512], F32, nbufs=2)
            ssv, ssv_b = A.tile("ssv", [128, 64], F32)
            rs_t2s = A.tile("rs2s", [128, 512], F32)
            S.dma("pool", wsTb, wsT_d, writes=[wsTb_b])
            S.dma("sp", bs_bc.rearrange("p g q -> p (g q)"), bs_d.partition_broadcast(128), writes=[bs_b])
            nwin = 0
            nwo = 0
            nu = 0
            nsv = 0
            for TB in range(2):
                T0 = TB * 1024
                rss1 = [rs_t, rs_t2s]
                for hb in range(2):
                    fm_stat(T0 + hb * 512, 512, sq_t, rss1[hb], [xTb[TB * 2 + hb]])

                def sgu_mod(hb):
                    fm_mod(T0 + hb * 512, 512, gms["gm_mix1"][0], 1, 0, hT1[:, :, hb * 512:(hb + 1) * 512], hT1_b[hb],
                           rss1[hb], tmp_t, [xTb[TB * 2 + hb]], gms["gm_mix1"][1])

                sgu_mod(0)
                def load_v(vb):
                    nonlocal nwin
                    sl = nwin % 2
                    nwin += 1
                    c0 = 3072 + vb * 512
                    S.dma("pool", wins[:, sl], win_d[:, c0:c0 + 512].rearrange("(k p) n -> p k n", p=128),
                          writes=[wins_b[sl]], semkey=f"wins{sl}")
                    S.dma("pool", bvs[:, sl, :], binv_d[:, vb * 512:(vb + 1) * 512], writes=[bvs_b[sl]], semkey=f"bvs{sl}")
                    return sl

                gst = {}
                ust = {}

                def load_g_dma(g):
                    nonlocal nwin, nwo
                    sl = nwo % 2
                    nwo += 1
                    wsl = nwin % 2
                    nwin += 1
                    S.dma("pool", wins[:, wsl, :, 0:384], win_d[:, g * 384:(g + 1) * 384].rearrange("(k p) n -> p k n", p=128),
                          writes=[wins_b[wsl]], semkey=f"wins{wsl}")
                    S.dma("pool", wouts[:, sl], wout_d[g * 384:(g + 1) * 384, :].rearrange("(c p) n -> p c n", p=128),
                          writes=[wouts_b[sl]], semkey=f"wouts{sl}")
                    gst[g] = (sl, wsl)

                vsl = {0: load_v(0)}
                for vb in range(6):
                    if vb + 1 < 6:
                        vsl[vb + 1] = load_v(vb + 1)
                    else:
                        load_g_dma(0)
                    sl = vsl[vb]
                    for i in range(8):
                        if vb == 0 and i == 4:
                            sgu_mod(1)
                        bk = i % 2
                        for k in range(8):
                            mm(S, psb(bk), hT1[:, k, i * 128:(i + 1) * 128], wins[:, sl, k, :], k == 0, False,
                               [hT1_b[i // 4], wins_b[sl]], [bank[bk]], False)
                        mm(S, psb(bk), onesb[0:1, :], bvs[0:1, sl, :], False, True, [onesb_b, bvs_b[sl]], [bank[bk]], True)
                        act(S, vraw[:, i, vb * 512:(vb + 1) * 512], psb(bk), AF.Gelu, [bank[bk]], [vraw_b[i]])
                        act(S, sq_t[0][:, 0, :], vraw[:, i, vb * 512:(vb + 1) * 512], AF.Square, [vraw_b[i]], [sq_t[1], ssv_b],
                            accum_out=ssv[:, i * 6 + vb:i * 6 + vb + 1])
                S.op("dve", lambda e: e.tensor_reduce(out=ssv[:, 48:56], in_=ssv[:, 0:48].rearrange("p (i v) -> p i v", v=6),
                                                      axis=AX.X, op=ALU.add), [ssv_b], [ssv_b])
                act(S, ssv[:, 56:64], ssv[:, 48:56], AF.Sqrt, [ssv_b], [ssv_b], bias=EPS, scale=1.0 / 3072)
                recip(S, ssv[:, 56:64], ssv[:, 56:64], [ssv_b], [ssv_b])
                rv = ssv[:, 56:64]
                its = [(g, half) for g in range(8) for half in range(2)]

                def load_g(g):
                    if g not in gst:
                        load_g_dma(g)
                    sl, wsl = gst[g]
                    tt(S, "pool", wsr[:, sl], wsTb[:, g, :].unsqueeze(1).broadcast_to([128, 8, 128]),
                       rv.unsqueeze(2).broadcast_to([128, 8, 128]), ALU.mult, [wsTb_b, ssv_b], [wsr_b[sl]])

                def u_part(n):
                    nonlocal nu
                    g, half = its[n]
                    sl, wsl = gst[g]
                    us = []
                    for c in range(3):
                        fc = g * 3 + c
                        bk = c % 2
                        u6 = nu % 6
                        nu += 1
                        for k in range(8):
                            mm(S, psb(bk), wins[:, wsl, k, c * 128:(c + 1) * 128], hT1[:, k, half * 512:(half + 1) * 512],
                               k == 0, k == 7, [wins_b[wsl], hT1_b[half]], [bank[bk]], k == 7)
                        act(S, uT[:, u6, :], psb(bk), AF.Gelu, [bank[bk], vecs_b], [uT_b[u6]],
                            bias=vecs[:, V_BINU + fc:V_BINU + fc + 1])
                        us.append(u6)
                    ust[n] = us

                def sv_part(n):
                    nonlocal nsv
                    g, half = its[n]
                    sl, wsl = gst[g]
                    us = ust[n]
                    for c in range(3):
                        fc = g * 3 + c
                        bk = 2 + (nsv % 2)
                        s2 = nsv % 2
                        nsv += 1
                        for i4 in range(4):
                            i = half * 4 + i4
                            mm(S, ps[:, bk, i4 * 128:(i4 + 1) * 128], vraw[:, i, fc * 128:(fc + 1) * 128], wsr[:, sl, i, :],
                               True, True, [vraw_b[i], wsr_b[sl]], [bank[bk]], i4 == 3)
                        stt(S, svt[:, s2, :].rearrange("p (a b) -> p a b", b=128), ps[:, bk, :].rearrange("p (a b) -> p a b", b=128),
                            vecs[:, V_VG + fc:V_VG + fc + 1], bs_bc[:, g, :].unsqueeze(1).broadcast_to([128, 4, 128]),
                            ALU.mult, ALU.add, [bank[bk], vecs_b, bs_b], [svt_b[s2]])
                        tt(S, "pool", mT[:, us[c], :], svt[:, s2, :], uT[:, us[c], :], ALU.mult, [svt_b[s2], uT_b[us[c]]],
                           [mT_b[us[c]]])

                def y_part(n):
                    g, half = its[n]
                    sl, wsl = gst[g]
                    us = ust.pop(n)
                    tb = TB * 2 + half
                    for j in range(8):
                        bk = 4 + (j % 3)
                        for c in range(3):
                            mm(S, psb(bk), wouts[:, sl, c, j * 128:(j + 1) * 128], mT[:, us[c], :], c == 0, c == 2,
                               [wouts_b[sl], mT_b[us[c]]], [bank[bk]], c == 2)
                        xv = xT[:, j, tb * 512:(tb + 1) * 512]
                        stt(S, xv, psb(bk), mod_s(1, 2, j), xv, ALU.mult, ALU.add, [bank[bk], modT[1][1], xTb[tb]], [xTb[tb]])

                load_g(0)
                u_part(0)
                sv_part(0)
                for n in range(len(its)):
                    if its[n][1] == 0 and its[n][0] + 1 < 8:
                        load_g(its[n][0] + 1)
                    if n + 1 < len(its):
                        u_part(n + 1)
                        sv_part(n + 1)
                    y_part(n)

        if stage >= 5:
            mlp(1)

        A.reset()
        fg_bc, fg_b = A.tile("fg_bc", [128, D], F32)
        S.dma("sp", fg_bc, finalg_d.partition_broadcast(128), writes=[fg_b])
        ost, ost_b = A.tile("ost", [128, 2, D], F32, nbufs=2)
        junkf, junkf_b = A.tile("junkf", [128, D], BF16)
        fs, fs_b = A.tile("fs", [128, 4], F32)
        outbufs = [Buf(f"out{i}") for i in range(2)]
        for t in range(16):
            o = t % 2
            for j in range(8):
                tr(S, ps[:, (o * 2) + j // 4, (j % 4) * 128:(j % 4 + 1) * 128], xT[:, j, t * 128:(t + 1) * 128], identf,
                   [xTb[t // 4], identf_b], [bank[o * 2 + j // 4]], j % 4 == 3)
            src2 = ps[:, o * 2:o * 2 + 2, :].rearrange("p a b -> p (a b)")
            bks = [bank[o * 2], bank[o * 2 + 1]]
            if stage >= 6:
                act(S, junkf, src2, AF.Square, bks, [junkf_b, fs_b], scale=1.0 / 32, accum_out=fs[:, 0:1])
                act(S, fs[:, 1:2], fs[:, 0:1], AF.Sqrt, [fs_b], [fs_b], bias=EPS, scale=1.0)
                recip(S, fs[:, 2:3], fs[:, 1:2], [fs_b], [fs_b])
                stt(S, ost[:, o, :], src2, fs[:, 2:3], fg_bc, ALU.mult, ALU.mult, bks + [fs_b, fg_b], [ost_b[o]])
            else:
                cp(S, "dve", ost[:, o, :], src2, bks, [ost_b[o]])
            S.dma("sp", out_d[t * 128:(t + 1) * 128, :], ost[:, o, :], reads=[ost_b[o]], writes=[outbufs[o]], semkey=f"out{o}")
        S.wait_all("sp", outbufs)
        S.emit(block)
    return nc


def _rope_tables():
    n = SEQ
    rows = np.repeat(np.arange(n // 64, dtype=np.int32), 64).astype(np.float32)
    cols = np.tile(np.arange(64, dtype=np.int32), n // 64).astype(np.float32)
    freqs = (1.0 / (np.float32(10000.0) ** (np.arange(0, 64, 2, dtype=np.float32) / np.float32(64)))).astype(np.float32)
    ang = np.concatenate([rows[:, None] * freqs, cols[:, None] * freqs], axis=-1).astype(np.float32)
    return np.cos(ang).astype(np.float32), np.sin(ang).astype(np.float32)


_NC_CACHE = {}


def make_in_maps(inp):
    f = lambda a: np.ascontiguousarray(np.asarray(a, dtype=np.float32))
    cos, sin = _rope_tables()

    def fm(v):
        v = np.asarray(v, dtype=np.float32)
        return v.reshape(-1, 128).T

    vecs = np.concatenate([
        fm(inp["ada_b"][0]), fm(inp["ada_b"][1]), fm(inp["mix_norm_g"][0]), fm(inp["mix_norm_g"][1]),
        fm(inp["mlp_norm_g"][0]), fm(inp["mlp_norm_g"][1]), fm(inp["sgu_b_in"][0][:3072]), fm(inp["sgu_v_g"][0])], axis=1)
    assert vecs.shape == (128, NVEC)
    shared = {
        "ada_w": f(inp["ada_w"]), "vecs": f(vecs), "mixg0": f(inp["mix_norm_g"][0]), "final_g": f(inp["final_g"]),
        "q_g": f(inp["attn_q_g"][0]), "k_g": f(inp["attn_k_g"][0]), "binv": f(inp["sgu_b_in"][0][3072:].reshape(1, 3072)),
        "b_s": f(np.asarray(inp["sgu_b_s"][0]).reshape(1024)), "wqkv": f(inp["attn_wqkv"][0]), "wo": f(inp["attn_wo"][0]),
        "w1": f(inp["mlp_w1"]), "w2": f(inp["mlp_w2"]), "w_in": f(inp["sgu_w_in"][0]), "w_out": f(inp["sgu_w_out"][0]),
        "wsT": f(np.transpose(np.asarray(inp["sgu_w_s"][0]), (2, 0, 1))), "ident": np.eye(128, dtype=np.float32),
    }
    maps = []
    ones_c = np.ones((CTX, 64), np.float32)
    zeros_c = np.zeros((CTX, 64), np.float32)
    for c in range(8):
        b, r = c // 4, c % 4
        x = np.asarray(inp["x"][b], dtype=np.float32)
        m = dict(shared)
        m["xs"] = f(np.roll(x, -r * T, axis=0))
        m["ctx"] = f(inp["ctx"][b])
        m["cs"] = f(np.concatenate([np.roll(cos, -r * T, axis=0), ones_c], axis=0))
        m["sn"] = f(np.concatenate([np.roll(sin, -r * T, axis=0), zeros_c], axis=0))
        m["cvec"] = f(np.stack([np.asarray(inp["c"][b]), np.asarray(inp["c_ctx"])]))
        maps.append(m)
    return maps


def kernel(**inputs):
    if "nc" not in _NC_CACHE:
        _NC_CACHE["nc"] = build()
    nc = _NC_CACHE["nc"]
    maps = make_in_maps(inputs)
    res = run_bass_kernel_spmd(nc, maps, core_ids=list(range(8)))
    out = np.empty((2, SEQ, D), np.float32)
    for c in range(8):
        b, r = c // 4, c % 4
        out[b, r * T:(r + 1) * T] = res.results[c]["out"]
    return out
```

```python
import numpy as np
import concourse.bass as bass
import concourse.mybir as mybir
from concourse.bass_utils import run_bass_kernel_spmd

F32 = mybir.dt.float32
BF16 = mybir.dt.bfloat16
AF = mybir.ActivationFunctionType
ALU = mybir.AluOpType
AX = mybir.AxisListType

D = 1024
KC = 8
T = 2048
SEQ = 8192
CTX = 256
NKT = 66
EPS = 1e-6
SBUF_BASE = 16576
SBUF_CAP = 212736


class Buf:
    __slots__ = ("name", "w", "r", "dsem", "dcnt")

    def __init__(self, name, pending=()):
        self.name = name
        self.w = None
        self.r = list(pending)
        self.dsem = None
        self.dcnt = 0


class Sched:
    ENGS = ("pe", "act", "dve", "pool", "sp")

    def __init__(self, nc):
        self.nc = nc
        self.ops = {e: [] for e in self.ENGS}
        self.sem = {e: nc.alloc_semaphore(name=f"s_{e}") for e in self.ENGS}
        self.cnt = {e: 0 for e in self.ENGS}
        self.waited = {}
        self.pending = {e: False for e in self.ENGS}
        self.dsems = {}

    def _deps(self, eng, reads, writes):
        deps = []
        for b in reads:
            if b.w is not None:
                deps.append(b.w)
        for b in writes:
            if b.w is not None:
                deps.append(b.w)
            deps.extend(b.r)
        waits = []
        for (sem, val, peng) in deps:
            if peng == eng and eng in ("pe", "sp"):
                continue
            key = (eng, id(sem))
            if self.waited.get(key, 0) >= val:
                continue
            self.waited[key] = val
            waits.append((sem, val))
        return waits

    @staticmethod
    def _commit(tok, reads, writes):
        for b in reads:
            b.r.append(tok)
        for b in writes:
            b.w = tok
            b.r = []

    def op(self, eng, fn, reads=(), writes=(), signal=True):
        waits = self._deps(eng, reads, writes)
        if signal:
            self.cnt[eng] += 1
            tok = (self.sem[eng], self.cnt[eng], eng)
            self.pending[eng] = False
        else:
            tok = (self.sem[eng], self.cnt[eng] + 1, eng)
            self.pending[eng] = True
        self.ops[eng].append((waits, fn, (self.sem[eng], 1) if signal else None))
        self._commit(tok, reads, writes)
        return tok

    def dma(self, q, out, in_, reads=(), writes=(), semkey=None):
        slot = writes[0] if writes else reads[0]
        if slot.dsem is None:
            key = semkey if semkey is not None else slot.name
            if key not in self.dsems:
                self.dsems[key] = [self.nc.alloc_semaphore(name=f"d_{key}"), 0]
            slot.dsem = self.dsems[key]
        waits = self._deps(q, reads, writes)
        slot.dsem[1] += 16
        tok = (slot.dsem[0], slot.dsem[1], None)
        self.ops[q].append((waits, lambda e, o=out, i=in_: e.dma_start(out=o, in_=i), (slot.dsem[0], 16)))
        self._commit(tok, reads, writes)
        return tok

    def wait_all(self, eng, bufs):
        waits = self._deps(eng, [], bufs)
        self.ops[eng].append((waits, None, None))

    def emit(self, block):
        nc = self.nc
        for e in self.ENGS:
            assert not self.pending[e], f"engine {e} has unsignalled trailing ops"
        engobj = {"pe": nc.tensor, "act": nc.scalar, "dve": nc.vector, "pool": nc.gpsimd, "sp": nc.sync}

        def run(ename):
            eng = engobj[ename]
            for waits, fn, inc in self.ops[ename]:
                for (sem, val) in waits:
                    eng.wait_ge(sem, val)
                if fn is None:
                    continue
                ins = fn(eng)
                if inc is not None:
                    ins.then_inc(inc[0], inc[1])

        @block.tensor
        def _(e):
            run("pe")

        @block.scalar
        def _(e):
            run("act")

        @block.vector
        def _(e):
            run("dve")

        @block.gpsimd
        def _(e):
            run("pool")

        @block.sync
        def _(e):
            run("sp")


class Arena:
    def __init__(self, nc, base, size, tag):
        self.nc, self.base, self.size, self.tag = nc, base, size, tag
        self.cur = 0
        self.bufs = []
        self.pending = []
        self.n = 0
        self.gen = 0

    def reset(self):
        best = {}
        for tok in self.pending:
            k = id(tok[0])
            if k not in best or best[k][1] < tok[1]:
                best[k] = tok
        for b in self.bufs:
            for tok in ([b.w] if b.w is not None else []) + b.r:
                k = id(tok[0])
                if k not in best or best[k][1] < tok[1]:
                    best[k] = tok
        self.pending = list(best.values())
        self.bufs = []
        self.cur = 0
        self.gen += 1

    def absorb(self, *others):
        best = {id(t[0]): t for t in self.pending}
        for o in others:
            for tok in list(o.pending) + [x for b in o.bufs for x in (([b.w] if b.w is not None else []) + b.r)]:
                k = id(tok[0])
                if k not in best or best[k][1] < tok[1]:
                    best[k] = tok
        self.pending = list(best.values())

    def buf(self, name):
        b = Buf(name, self.pending)
        self.bufs.append(b)
        return b

    def tile(self, name, shape, dtype, nbufs=1):
        esz = 4 if dtype == F32 else 2
        nbytes = int(np.prod(shape[1:])) * esz
        nbytes = (nbytes + 63) // 64 * 64
        off = self.base + self.cur
        self.cur += nbytes
        assert self.cur <= self.size, f"arena {self.tag} overflow at {name}: {self.cur} > {self.size}"
        self.n += 1
        t = self.nc.alloc_sbuf_tensor_at(f"{self.tag}{self.gen}_{name}_{self.n}", list(shape), dtype, offset=off).ap()
        if nbufs == 1:
            return t, self.buf(name)
        return t, [self.buf(f"{name}{i}") for i in range(nbufs)]


def mm(S, out, lhsT, rhs, start, stop, reads, writes, signal, tile_position=None):
    if tile_position is not None:
        return S.op("pe", lambda e: e.matmul(out, lhsT=lhsT, rhs=rhs, start=start, stop=stop, tile_position=tile_position),
                    reads, writes, signal)
    return S.op("pe", lambda e: e.matmul(out, lhsT=lhsT, rhs=rhs, start=start, stop=stop), reads, writes, signal)


def tr(S, out, in_, ident, reads, writes, signal):
    return S.op("pe", lambda e: e.transpose(out=out, in_=in_, identity=ident), reads, writes, signal)


def act(S, out, in_, func, reads, writes, **kw):
    return S.op("act", lambda e: e.activation(out=out, in_=in_, func=func, **kw), reads, writes)


def stt(S, out, in0, scalar, in1, op0, op1, reads, writes):
    return S.op("dve", lambda e: e.scalar_tensor_tensor(out=out, in0=in0, scalar=scalar, in1=in1, op0=op0, op1=op1),
                reads, writes)


def tt(S, eng, out, in0, in1, op, reads, writes):
    return S.op(eng, lambda e: e.tensor_tensor(out=out, in0=in0, in1=in1, op=op), reads, writes)


def ts(S, eng, out, in0, s1, s2, op0, op1, reads, writes):
    if op1 is None:
        return S.op(eng, lambda e: e.tensor_scalar(out=out, in0=in0, scalar1=s1, scalar2=None, op0=op0), reads, writes)
    return S.op(eng, lambda e: e.tensor_scalar(out=out, in0=in0, scalar1=s1, scalar2=s2, op0=op0, op1=op1), reads, writes)


def cp(S, eng, out, in_, reads, writes):
    if eng == "act":
        return S.op("act", lambda e: e.copy(out=out, in_=in_), reads, writes)
    return S.op(eng, lambda e: e.tensor_copy(out=out, in_=in_), reads, writes)


def recip(S, out, in_, reads, writes):
    return S.op("dve", lambda e: e.reciprocal(out=out, in_=in_), reads, writes)


V_ADAB0, V_ADAB1, V_MIXG0, V_MIXG1, V_MLPG0, V_MLPG1, V_BINU, V_VG = 0, 48, 96, 104, 112, 120, 128, 152
NVEC = 176


def build(stage=99):
    nc = bass.Bass("TRN2", target_bir_lowering=False)

    def din(name, shape):
        return nc.dram_tensor(name, list(shape), F32, kind="ExternalInput").ap()

    xs = din("xs", [SEQ, D])
    ctx = din("ctx", [CTX, D])
    cs_d = din("cs", [NKT * 128, 64])
    sn_d = din("sn", [NKT * 128, 64])
    cvec = din("cvec", [2, D])
    ada_w = din("ada_w", [2, D, 6 * D])
    vecs_d = din("vecs", [128, NVEC])
    mixg0_d = din("mixg0", [D])
    finalg_d = din("final_g", [D])
    qg_d = din("q_g", [128])
    kg_d = din("k_g", [128])
    binv_d = din("binv", [1, 3072])
    bs_d = din("b_s", [1024])
    wqkv_d = din("wqkv", [D, 1536])
    wo_d = din("wo", [D, D])
    w1_d = din("w1", [2, D, 4 * D])
    w2_d = din("w2", [2, 4 * D, D])
    win_d = din("w_in", [D, 6144])
    wout_d = din("w_out", [3072, D])
    wsT_d = din("wsT", [128, 8, 128])
    ident_d = din("ident", [128, 128])
    out_d = nc.dram_tensor("out", [T, D], F32, kind="ExternalOutput").ap()

    S = Sched(nc)
    SM = Arena(nc, SBUF_BASE, 3584, "sm")
    modT = []
    for l in range(2):
        modT.append(SM.tile(f"modT{l}", [128, 48, 2], F32))
    vecs, vecs_b = SM.tile("vecs", [128, NVEC], F32)
    identf, identf_b = SM.tile("identf", [128, 128], F32)
    identb, identb_b = SM.tile("identb", [128, 128], BF16)
    onesf, onesf_b = SM.tile("onesf", [128, 128], F32)
    onesb, onesb_b = SM.tile("onesb", [128, 128], BF16)
    scT, scT_b = SM.tile("scT", [128, 8, 2], F32)
    scTb, scTb_b = SM.tile("scTb", [128, 8, 2], BF16)
    cT, cT_b = SM.tile("cT", [128, 2, 8], F32)
    gms = {}
    for nm in ("gm_mlp0", "gm_mix1", "gm_mlp1", "gm0x", "gm0c"):
        gms[nm] = SM.tile(nm, [128, 8], F32)
    SM_SZ = 3584
    FREE = SBUF_CAP - SM_SZ
    TOP_SZ = 2 * 33792 + 32768
    BOT = Arena(nc, SBUF_BASE + SM_SZ, FREE - TOP_SZ, "bot")
    TOP = Arena(nc, SBUF_BASE + SM_SZ + FREE - TOP_SZ, TOP_SZ, "top")
    A = Arena(nc, SBUF_BASE + SM_SZ + 65536, FREE - 65536, "hi")

    ps = nc.alloc_psum_tensor("ps", [128, 8, 512], F32).ap()
    bank = [Buf(f"bank{i}") for i in range(8)]

    def psb(i):
        return ps[:, i, :]

    def psb16(i):
        return ps[:, i, :].bitcast(BF16)

    with nc.Block() as block:
        S.dma("sp", identf, ident_d, writes=[identf_b])
        S.dma("sp", vecs, vecs_d, writes=[vecs_b])
        S.dma("sp", cT, cvec.rearrange("j (p k) -> p j k", k=8), writes=[cT_b])
        cp(S, "dve", identb, identf, [identf_b], [identb_b])
        S.op("pool", lambda e: e.memset(onesf, 1.0), [], [onesf_b])
        S.op("pool", lambda e: e.memset(onesb, 1.0), [], [onesb_b])
        act(S, scT.rearrange("p k j -> p j k"), cT, AF.Silu, [cT_b], [scT_b])
        cp(S, "dve", scTb, scT, [scT_b], [scTb_b])

        def mods_gen(l, slots, chunk_cols, bks, col0, col1, tag, prime=False):
            mt, mt_b = modT[l]
            nch = (col1 - col0) // chunk_cols
            src = ada_w[l].rearrange("(p k) n -> p k n", k=8)
            jper = chunk_cols // 128
            off = V_ADAB0 if l == 0 else V_ADAB1

            def load(ci):
                buf, bb = slots[ci % 2]
                S.dma("pool", buf[:, :, 0:chunk_cols], src[:, :, col0 + ci * chunk_cols:col0 + (ci + 1) * chunk_cols],
                      writes=[bb], semkey=f"ada{tag}_{ci % 2}")

            load(0)
            if prime:
                yield
            for ci in range(nch):
                if ci + 1 < nch:
                    load(ci + 1)
                buf, bb = slots[ci % 2]
                bk = bks[ci % len(bks)]
                j0 = col0 // 128 + ci * jper
                for jj in range(jper):
                    for k in range(8):
                        mm(S, ps[:, bk, 2 * jj:2 * jj + 2], buf[:, k, jj * 128:(jj + 1) * 128], scTb[:, k, :], k == 0, k == 7,
                           [bb, scTb_b], [bank[bk]], jj == jper - 1 and k == 7)
                tt(S, "dve", mt[:, j0:j0 + jper, :], ps[:, bk, 0:2 * jper].rearrange("p (j c) -> p j c", c=2),
                   vecs[:, off + j0:off + j0 + jper].unsqueeze(2).broadcast_to([128, jper, 2]), ALU.add,
                   [bank[bk], vecs_b], [mt_b])
                yield

        ada_slots0 = [BOT.tile(f"ada0_{i}", [128, 8, 512], BF16) for i in range(2)]
        for _ in mods_gen(0, ada_slots0, 512, [6, 7], 0, 2048, "s"):
            pass

        def gm_make(nm, gcol, l, m):
            g, g_b = gms[nm]
            mt, mt_b = modT[l]
            stt(S, g, mt[:, m * 8:(m + 1) * 8, 0], 1.0, vecs[:, gcol:gcol + 8], ALU.add, ALU.mult, [mt_b, vecs_b], [g_b])


        def mod_s(l, m, j):
            return modT[l][0][:, m * 8 + j, 0:1]

        KT, KT_b = TOP.tile("KT", [128, 2, NKT * 128], BF16)
        Vs, Vs_b = TOP.tile("Vs", [128, NKT, 256], BF16)
        QT, _qtb = TOP.tile("QT", [128, 8, T], BF16)
        QT_b = [[TOP.buf(f"QT{q}_{h}") for h in range(8)] for q in range(4)]
        BOT.reset()
        wq, wq_b = BOT.tile("wq", [128, 8, 1024], BF16)
        wkv, wkv_b = BOT.tile("wkv", [128, 8, 512], BF16)
        S.dma("pool", wkv, wqkv_d[:, 1024:1536].rearrange("(k p) n -> p k n", p=128), writes=[wkv_b])
        S.dma("pool", wq, wqkv_d[:, 0:1024].rearrange("(k p) n -> p k n", p=128), writes=[wq_b])
        wkvc, wkvc_b = BOT.tile("wkvc", [128, 8, 512], BF16)
        bkv_bc = [BOT.tile(f"bkv_bc{c}", [128, 512], F32) for c in range(2)]
        bq_bc, bq_b = BOT.tile("bq_bc", [128, 1024], F32)
        kqg, kqg_b = BOT.tile("kqg", [128, 10, 128], F32)
        for hh in range(10):
            S.dma("sp", kqg[:, hh, :], (kg_d if hh < 2 else qg_d).partition_broadcast(128), writes=[kqg_b], semkey="kqg")
        prep_keep_n = len(BOT.bufs)
        prep_keep_cur = BOT.cur
        LO = BOT

        shiftB, shiftB_b = LO.tile("shiftB", [128, 8, 128], BF16)
        src2 = ps[:, 0:2, :].rearrange("p a b -> p (a b)")
        for col in range(2):
            for j in range(8):
                S.op("dve", lambda e, j=j, col=col: e.tensor_scalar(
                    out=shiftB[:, j, :], in0=onesb, scalar1=modT[0][0][:, j, col:col + 1], scalar2=None, op0=ALU.mult),
                    [onesb_b, modT[0][1]], [shiftB_b])
            for j in range(8):
                mm(S, psb(2), shiftB[:, j, :], wkv[:, j, :], j == 0, j == 7, [shiftB_b, wkv_b], [bank[2]], j == 7)
            cp(S, "act", bkv_bc[col][0], psb(2), [bank[2]], [bkv_bc[col][1]])
            if col == 0:
                for half in range(2):
                    for j in range(8):
                        mm(S, psb(half), shiftB[:, j, :], wq[:, j, half * 512:(half + 1) * 512], j == 0, j == 7,
                           [shiftB_b, wq_b], [bank[half]], j == 7)
                cp(S, "act", bq_bc, src2, [bank[0], bank[1]], [bq_b])
        for col, nm in ((0, "gm0x"), (1, "gm0c")):
            g, g_b = gms[nm]
            stt(S, g, modT[0][0][:, 8:16, col], 1.0, vecs[:, V_MIXG0:V_MIXG0 + 8], ALU.add, ALU.mult, [modT[0][1], vecs_b], [g_b])
        for k in range(8):
            ts(S, "dve", wkvc[:, k, :], wkv[:, k, :], gms["gm0c"][0][:, k:k + 1], None, ALU.mult, None,
               [wkv_b, gms["gm0c"][1]], [wkvc_b])
        for k in range(8):
            ts(S, "dve", wkv[:, k, :], wkv[:, k, :], gms["gm0x"][0][:, k:k + 1], None, ALU.mult, None,
               [wkv_b, gms["gm0x"][1]], [wkv_b])
            ts(S, "dve", wq[:, k, :], wq[:, k, :], gms["gm0x"][0][:, k:k + 1], None, ALU.mult, None,
               [wq_b, gms["gm0x"][1]], [wq_b])

        keep = BOT.bufs[:prep_keep_n]
        BOT.reset()
        BOT.bufs = keep
        BOT.cur = prep_keep_cur

        def ring(ar, name, shape, dtype, n):
            ap, bufs = ar.tile(name, [shape[0], n] + list(shape[1:]), dtype, nbufs=n)
            if n == 1:
                bufs = [bufs]
            return [ap[:, i] for i in range(n)], bufs

        NX = 3
        NTAB = 6
        xt, xt_b = ring(LO, "xt", [128, D], F32, NX)
        ctab, ctab_b = ring(LO, "ctab", [128, 64], F32, NTAB)
        stab, stab_b = ring(LO, "stab", [128, 64], F32, NTAB)
        junk, junk_b = LO.tile("junk", [128, D], BF16)
        NXB = 3
        xb, xb_b = ring(LO, "xb", [128, D], BF16, NXB)
        hTs, hTs_b = ring(LO, "hTs", [128, 8, 128], BF16, 2)
        kq32, kq32_b = ring(LO, "kq32", [128, 10, 128], F32, 2)
        sqj, sqj_b = ring(LO, "sqj", [128, 10, 128], BF16, 1)
        rr32, rr32_b = ring(LO, "rr32", [128, 10, 128], F32, 2)
        rpb, rpb_b = LO.tile("rpb", [128, 10, 64], F32)
        rdb, rdb_b = LO.tile("rdb", [128, 10, 64], F32)
        rr32e_b = [LO.buf(f"rr32e{i}") for i in range(2)]
        rr32o_b = [LO.buf(f"rr32o{i}") for i in range(2)]
        krq, krq_b = ring(LO, "krq", [128, 10, 128], BF16, 1)
        ms, ms_b = ring(LO, "ms", [128, 4], F32, 4)
        ss, ss_b = ring(LO, "ss", [128, 32], F32, 3)

        def prep_load(t):
            s = t % NX
            s6 = t % NTAB
            src = xs[t * 128:(t + 1) * 128, :] if t < 64 else ctx[(t - 64) * 128:(t - 63) * 128, :]
            S.dma("sp", xt[s], src, writes=[xt_b[s]], semkey=f"xt{s}")
            S.dma("pool", xb[t % NXB], src, writes=[xb_b[t % NXB]], semkey=f"xb{t % NXB}")
            S.dma("sp", ctab[s6], cs_d[t * 128:(t + 1) * 128, :], writes=[ctab_b[s6]], semkey=f"ct{s6}")
            S.dma("sp", stab[s6], sn_d[t * 128:(t + 1) * 128, :], writes=[stab_b[s6]], semkey=f"st{s6}")

        def st1(t):
            s, col, m, mb = t % NX, (0 if t < 64 else 1), ms[t % 4], ms_b[t % 4]
            act(S, junk, xt[s], AF.Square, [xt_b[s]], [junk_b, mb], scale=1.0 / 32, accum_out=m[:, 0:1])
            act(S, m[:, 1:2], m[:, 0:1], AF.Sqrt, [mb], [mb], bias=EPS, scale=1.0)

        def st2_pe(t):
            p = t % 2
            hb = 4 + p
            for j in range(8):
                tr(S, psb16(hb)[:, j * 128:(j + 1) * 128], xb[t % NXB][:, j * 128:(j + 1) * 128], identb,
                   [xb_b[t % NXB], identb_b], [bank[hb]], j == 7)

        def st2_dve(t):
            m, mb = ms[t % 4], ms_b[t % 4]
            recip(S, m[:, 2:3], m[:, 1:2], [mb], [mb])

        def st2_act(t):
            p = t % 2
            hb = 4 + p
            cp(S, "act", hTs[p], psb16(hb).rearrange("p (a b) -> p a b", b=128), [bank[hb]], [hTs_b[p]])

        def st3_pe(t):
            p = t % 2
            bk = 2 + p
            wk_, wkb_ = (wkv, wkv_b) if t < 64 else (wkvc, wkvc_b)
            for k in range(8):
                mm(S, psb(bk), hTs[p][:, k, :], wk_[:, k, :], k == 0, k == 7, [hTs_b[p], wkb_], [bank[bk]], k == 7)
            if t < 16:
                for half in range(2):
                    for k in range(8):
                        mm(S, psb(half), hTs[p][:, k, :], wq[:, k, half * 512:(half + 1) * 512], k == 0, k == 7,
                           [hTs_b[p], wq_b], [bank[half]], k == 7)

        def st3_dve(t):
            m, mb, p, col = ms[t % 4], ms_b[t % 4], t % 2, (0 if t < 64 else 1)
            bk = 2 + p
            kq = kq32[p]
            bkv, bkvb = bkv_bc[col]
            stt(S, kq[:, 0:2, :], ps[:, bk, 0:256].rearrange("p (h d) -> p h d", d=128), m[:, 2:3],
                bkv[:, 0:256].rearrange("p (h d) -> p h d", d=128), ALU.mult, ALU.add, [bank[bk], mb, bkvb], [kq32_b[p]])
            stt(S, Vs[:, t, :], ps[:, bk, 256:512], m[:, 2:3], bkv[:, 256:512], ALU.mult, ALU.add, [bank[bk], mb, bkvb], [Vs_b])
            if t < 16:
                stt(S, kq[:, 2:10, :], ps[:, 0:2, :].rearrange("p a (b c) -> p (a b) c", c=128), m[:, 2:3],
                    bq_bc.rearrange("p (h d) -> p h d", d=128), ALU.mult, ALU.add, [bank[0], bank[1], mb, bq_b], [kq32_b[p]])

        def st3_act(t):
            p = t % 2
            nh = 10 if t < 16 else 2
            kq = kq32[p]
            act(S, sqj[0][:, 0:nh, :], kq[:, 0:nh, :], AF.Square, [kq32_b[p]], [sqj_b[0]])
            tt(S, "pool", kq[:, 0:nh, :], kq[:, 0:nh, :], kqg[:, 0:nh, :], ALU.mult, [kq32_b[p], kqg_b, sqj_b[0]], [kq32_b[p]])

        def st4(t):
            p, own, s6 = t % 2, t < 16, t % NTAB
            nh = 10 if own else 2
            sv, svb = ss[t % 3], ss_b[t % 3]
            S.op("dve", lambda e: e.tensor_reduce(out=sv[:, 0:nh], in_=sqj[0][:, 0:nh, :], axis=AX.X, op=ALU.add),
                 [sqj_b[0]], [svb])
            act(S, sv[:, 16:16 + nh], sv[:, 0:nh], AF.Sqrt, [svb], [svb], bias=EPS, scale=1.0 / 128)
            src = kq32[p][:, 0:nh, :]
            dst = rr32[p][:, 0:nh, :]
            x1, x2 = src[:, :, 0::2], src[:, :, 1::2]
            cb = ctab[s6].unsqueeze(1).broadcast_to([128, nh, 64])
            sb = stab[s6].unsqueeze(1).broadcast_to([128, nh, 64])
            rd = [kq32_b[p], ctab_b[s6], stab_b[s6]]
            b_ = rpb[:, 0:nh, :]
            tt(S, "pool", dst[:, :, 0::2], x1, cb, ALU.mult, rd, [rr32e_b[p]])
            tt(S, "pool", b_, x2, sb, ALU.mult, rd, [rpb_b])
            tt(S, "pool", dst[:, :, 0::2], dst[:, :, 0::2], b_, ALU.subtract, [rpb_b, rr32e_b[p]], [rr32e_b[p]])
            b_ = rdb[:, 0:nh, :]
            tt(S, "dve", dst[:, :, 1::2], x1, sb, ALU.mult, rd, [rr32o_b[p]])
            tt(S, "dve", b_, x2, cb, ALU.mult, rd, [rdb_b])
            tt(S, "dve", dst[:, :, 1::2], dst[:, :, 1::2], b_, ALU.add, [rdb_b, rr32o_b[p]], [rr32o_b[p]])

        def st5_dve(t):
            p = t % 2
            nh = 10 if t < 16 else 2
            sv, svb = ss[t % 3], ss_b[t % 3]
            recip(S, sv[:, 16:16 + nh], sv[:, 16:16 + nh], [svb], [svb])
            tt(S, "dve", krq[0][:, 0:nh, :], rr32[p][:, 0:nh, :], sv[:, 16:16 + nh].unsqueeze(2).broadcast_to([128, nh, 128]),
               ALU.mult, [rr32e_b[p], rr32o_b[p], svb], [krq_b[0]])

        def st5_pe(t):
            for h in range(2):
                tr(S, psb16(7)[:, h * 128:(h + 1) * 128], krq[0][:, h, :], identb, [krq_b[0], identb_b], [bank[7]], h == 1)
            if t < 16:
                for h in range(8):
                    tr(S, psb16(6)[:, h * 128:(h + 1) * 128], krq[0][:, 2 + h, :], identb, [krq_b[0], identb_b], [bank[6]], h == 7)

        def st5_act(t):
            cp(S, "act", KT[:, :, t * 128:(t + 1) * 128], psb16(7)[:, 0:256].rearrange("p (h d) -> p h d", d=128),
               [bank[7]], [KT_b])
            if t < 16:
                cp(S, "act", QT[:, :, t * 128:(t + 1) * 128], psb16(6).rearrange("p (h d) -> p h d", d=128),
                   [bank[6]], QT_b[t // 4])

        def ok(t):
            return 0 <= t < NKT

        prep_load(0)
        prep_load(1)
        for it in range(NKT + 4):
            t1, t2, t3, t4, t5 = it, it - 1, it - 2, it - 3, it - 4
            if ok(t5):
                st5_dve(t5)
            if ok(t4):
                st4(t4)
            if ok(t1):
                st1(t1)
            if ok(t3):
                st3_pe(t3)
            if ok(t2):
                st2_pe(t2)
            if ok(t5):
                st5_pe(t5)
            if ok(t3):
                st3_dve(t3)
            if ok(t2):
                st2_dve(t2)
                st2_act(t2)
            if ok(t5):
                st5_act(t5)
            if ok(t3):
                st3_act(t3)
            if it + 2 < NKT:
                prep_load(it + 2)

        def fm_stat(tok0, ntok, sq_t, rs_t, xbufs, sbank=7):
            sq, sq_b = sq_t
            rs, rs_b = rs_t
            xv = xT[:, :, tok0:tok0 + ntok]
            act(S, sq[:, :, 0:ntok], xv, AF.Square, xbufs, [sq_b])
            for j in range(8):
                mm(S, ps[:, sbank, 0:ntok], onesb, sq[:, j, 0:ntok], j == 0, j == 7, [onesb_b, sq_b], [bank[sbank]], j == 7)
            act(S, rs[:, 0:ntok], ps[:, sbank, 0:ntok], AF.Sqrt, [bank[sbank]], [rs_b], bias=EPS, scale=1.0 / D)
            recip(S, rs[:, 0:ntok], rs[:, 0:ntok], [rs_b], [rs_b])

        def fm_mod(tok0, ntok, gm_ap, l, m_shift, hT_out, hT_out_b, rs_t, tmp_t, xbufs, gm_b=None):
            rs, rs_b = rs_t
            tmp, tmp_b = tmp_t
            for j in range(8):
                i2 = j % 2
                stt(S, tmp[:, i2, 0:ntok], xT[:, j, tok0:tok0 + ntok], gm_ap[:, j:j + 1], rs[:, 0:ntok], ALU.mult, ALU.mult,
                    xbufs + [rs_b] + ([gm_b] if gm_b is not None else []), [tmp_b[i2]])
                act(S, hT_out[:, j, :], tmp[:, i2, 0:ntok], AF.Identity, [tmp_b[i2], modT[l][1]], [hT_out_b],
                    bias=mod_s(l, m_shift, j), scale=1.0)

        def fm_norm(ar, tok0, ntok, gm_ap, l, m_shift, hT_out, hT_out_b, sq_t, rs_t, tmp_t, xbufs, gm_b=None):
            fm_stat(tok0, ntok, sq_t, rs_t, xbufs)
            fm_mod(tok0, ntok, gm_ap, l, m_shift, hT_out, hT_out_b, rs_t, tmp_t, xbufs, gm_b)

        if stage >= 2:
            BOT.reset()
            xT, _x = BOT.tile("xT", [128, KC, T], F32)
            xTb = [BOT.buf(f"xT{i}") for i in range(4)]
            wo, wo_b = BOT.tile("wo", [128, 8, 1024], BF16)
            S.dma("pool", wo, wo_d.rearrange("(h p) n -> p h n", p=128), writes=[wo_b])
            NPT = 8
            G = 6
            PT, PT_b = BOT.tile("PT", [128, NPT, 512], BF16, nbufs=NPT)
            rcp, rcp_b = BOT.tile("rcp", [128, 512], F32)
            Rsb, Rsb_b = BOT.tile("Rsb", [128, 2, 512], F32, nbufs=2)
            xr, xr_b = BOT.tile("xr", [128, 1, D], F32, nbufs=2)
            xr_b = [xr_b, xr_b] if not isinstance(xr_b, list) else xr_b
            ada_slots1 = [BOT.tile(f"ada1_{i}", [128, 8, 128], BF16) for i in range(2)]
            acc, acc_b = BOT.tile("acc", [128, 2, 512], F32, nbufs=2)
            NDVE = 3

            def _chain():
                yield from mods_gen(0, ada_slots1, 128, [7], 2048, 6144, "a", prime=True)
            mods1 = _chain()
            next(mods1, None)
            SCALE = 128.0 ** -0.5
            step = 0
            nxr = 0
            nunit = 0
            pend = []
            wo_pending = None

            def epi_a(u, rbk):
                ts(S, "dve", Rsb[0:96, u % 2, :], ps[0:96, rbk, :], 1.0 / 32, None, ALU.mult, None, [bank[rbk]], [Rsb_b[u % 2]])

            def epi_b(u, ob, qv, qbuf):
                mm(S, psb(7), onesf, acc[:, u % 2, :], True, False, [onesf_b, acc_b[u % 2]], [bank[7]], False)
                mm(S, psb(7), onesf[0:96, :], Rsb[0:96, u % 2, :], False, True, [onesf_b, Rsb_b[u % 2]], [bank[7]], True)
                recip(S, rcp, psb(7), [bank[7]], [rcp_b])
                tt(S, "dve", qv, psb(ob), rcp, ALU.mult, [bank[ob], rcp_b], [qbuf])

            for qb in range(4):
                tok0 = qb * 512
                def make_xT_block(qb=qb):
                    for i in range(4):
                        tg = qb * 4 + i
                        sl = 0
                        S.dma("sp", xr[:, sl, :], xs[tg * 128:(tg + 1) * 128, :], writes=[xr_b[sl]], semkey=f"xr{sl}")
                        for half in range(2):
                            for jj in range(4):
                                j = half * 4 + jj
                                tr(S, ps[:, 7, jj * 128:(jj + 1) * 128], xr[:, sl, j * 128:(j + 1) * 128], identf,
                                   [xr_b[sl], identf_b], [bank[7]], jj == 3)
                            cp(S, "dve", xT[:, half * 4:(half + 1) * 4, tg * 128:(tg + 1) * 128],
                               ps[:, 7, :].rearrange("p (b c) -> p b c", c=128), [bank[7]], [xTb[qb]])
                            yield

                xgen = make_xT_block()

                for h in range(8):
                    kvh = h // 4
                    ob = 4 + (h % 2)
                    rbk = 6
                    qv = QT[:, h, tok0:tok0 + 512]
                    qbuf = QT_b[qb][h]

                    def s_mm(kt):
                        sb_ = (step + kt) % 4
                        mm(S, psb(sb_), KT[:, kvh, kt * 128:(kt + 1) * 128], qv, True, True,
                           [KT_b, qbuf], [bank[sb_]], True)

                    s_mm(0)
                    s_mm(1)
                    s_mm(2)
                    for g0 in range(0, NKT, G):
                        for kt in range(g0, g0 + G):
                            sb_ = (step + kt) % 4
                            pb = (step + kt) % NPT
                            act(S, PT[:, pb, :], psb(sb_), AF.Exp, [bank[sb_]], [PT_b[pb]], scale=SCALE)
                            if kt + 3 < NKT:
                                s_mm(kt + 3)
                            mm(S, psb(ob), Vs[:, kt, kvh * 128:(kvh + 1) * 128], PT[:, pb, :], kt == 0, kt == NKT - 1,
                               [Vs_b, PT_b[pb]], [bank[ob]], kt == NKT - 1)
                        ndve = 0 if g0 == NKT - G else NDVE
                        for i in range(G):
                            kt = g0 + i
                            pb = (step + kt) % NPT
                            if i < ndve:
                                a_ = acc[:, nunit % 2, :]
                                if g0 == 0 and i == 0:
                                    cp(S, "dve", a_, PT[:, pb, :], [PT_b[pb]], [acc_b[nunit % 2]])
                                else:
                                    tt(S, "dve", a_, a_, PT[:, pb, :], ALU.add, [PT_b[pb], acc_b[nunit % 2]], [acc_b[nunit % 2]])
                            else:
                                cg = (i - ndve) % 3
                                last_use = (g0 == NKT - G) and (i - ndve) >= (G - ndve) - 3
                                mm(S, ps[32 * cg:32 * cg + 32, rbk, :], onesb[:, 0:32], PT[:, pb, :], g0 == 0, last_use,
                                   [onesb_b, PT_b[pb]], [bank[rbk]], i == G - 1, tile_position=(0, 32 * cg))
                        if g0 == 2 * G and pend:
                            epi_b(*pend.pop(0))
                        if g0 in (3 * G, 5 * G, 7 * G, 9 * G):
                            next(mods1, None)
                        if wo_pending is not None and g0 in (1 * G, 4 * G, 6 * G, 8 * G):
                            if next(wo_pending, "done") == "done":
                                wo_pending = None
                    step += NKT
                    next(xgen, None)
                    epi_a(nunit, rbk)
                    pend.append((nunit, ob, qv, qbuf))
                    nunit += 1
                for _ in xgen:
                    pass
                while pend:
                    epi_b(*pend.pop(0))
                def wo_gen(qb=qb, tok0=tok0, last=(qb == 3)):
                    for j in range(8):
                        wb = (7, 6, 4, 5)[j % 4] if last else 7
                        for h in range(8):
                            mm(S, psb(wb), wo[:, h, j * 128:(j + 1) * 128], QT[:, h, tok0:tok0 + 512], h == 0, h == 7,
                               [wo_b, QT_b[qb][h]], [bank[wb]], h == 7)
                        stt(S, xT[:, j, tok0:tok0 + 512], psb(wb), mod_s(0, 2, j), xT[:, j, tok0:tok0 + 512], ALU.mult, ALU.add,
                            [bank[wb], modT[0][1], xTb[qb]], [xTb[qb]])
                        yield

                if qb == 3:
                    for _ in wo_gen():
                        pass
                else:
                    wo_pending = wo_gen()

        if stage < 2:
            BOT.reset()
            xT, _x = BOT.tile("xT", [128, KC, T], F32)
            xTb = [BOT.buf(f"xT{i}") for i in range(4)]
            xr, xr_b = BOT.tile("xr", [128, 2, D], F32, nbufs=2)
            for tg in range(16):
                sl = tg % 2
                S.dma("sp", xr[:, sl, :], xs[tg * 128:(tg + 1) * 128, :], writes=[xr_b[sl]], semkey=f"xr{sl}")
                for half in range(2):
                    for jj in range(4):
                        j = half * 4 + jj
                        tr(S, ps[:, 7, jj * 128:(jj + 1) * 128], xr[:, sl, j * 128:(j + 1) * 128], identf,
                           [xr_b[sl], identf_b], [bank[7]], jj == 3)
                    cp(S, "dve", xT[:, half * 4:(half + 1) * 4, tg * 128:(tg + 1) * 128],
                       ps[:, 7, :].rearrange("p (b c) -> p b c", c=128), [bank[7]], [xTb[tg // 4]])
        if stage >= 2:
            for _ in mods1:
                pass
            gm_make("gm_mlp0", V_MLPG0, 0, 4)
        A.absorb(BOT, TOP)

        def mlp(l):
            A.reset()
            hT, hT_b = A.tile("hT", [128, 8, T], BF16, nbufs=4)
            sq_t = A.tile("sq", [128, 8, 512], BF16)
            rs_t = A.tile("rs", [128, 512], F32)
            tmp_t = A.tile("tmp", [128, 2, 512], F32, nbufs=2)
            NW = 3
            w1s, w1s_b = A.tile("w1s", [128, NW, 8, 512], BF16, nbufs=NW)
            w2s, w2s_b = A.tile("w2s", [128, NW, 4, 1024], BF16, nbufs=NW)
            rr, rr_b = A.tile("rr", [128, 2, 512], F32, nbufs=2)
            aT, aT_b = A.tile("aT", [128, 8, 512], BF16, nbufs=8)
            gm, gm_buf = gms["gm_mlp0" if l == 0 else "gm_mlp1"]

            def load_w(f):
                sl = f % NW
                S.dma("pool", w1s[:, sl], w1_d[l][:, f * 512:(f + 1) * 512].rearrange("(k p) n -> p k n", p=128),
                      writes=[w1s_b[sl]], semkey=f"w1s{sl}")
                S.dma("pool", w2s[:, sl], w2_d[l][f * 512:(f + 1) * 512, :].rearrange("(c p) n -> p c n", p=128),
                      writes=[w2s_b[sl]], semkey=f"w2s{sl}")

            load_w(0)
            load_w(1)
            load_w(2)
            sq_t2 = A.tile("sq2", [128, 8, 512], BF16)
            rs_t2 = A.tile("rs2", [128, 512], F32)
            sqs, rss = [sq_t, sq_t2], [rs_t, rs_t2]

            def n_stat(tb):
                fm_stat(tb * 512, 512, sqs[tb % 2], rss[tb % 2], [xTb[tb]], sbank=6 + tb % 2)

            def n_mod(tb):
                fm_mod(tb * 512, 512, gm, l, 3, hT[:, :, tb * 512:(tb + 1) * 512], hT_b[tb], rss[tb % 2], tmp_t, [xTb[tb]], gm_buf)

            n_stat(0)
            n_stat(1)
            n_mod(0)
            hooks = {0: [lambda: n_mod(1), lambda: n_stat(2)], 1: [lambda: n_mod(2), lambda: n_stat(3)], 2: [lambda: n_mod(3)]}
            units = [(f, tb) for tb in range(4) for f in range(3)] + [(f, tb) for f in range(3, 8) for tb in range(4)]
            last_use = {}
            for n_, (f_, tb_) in enumerate(units):
                last_use[f_] = n_
            asets = {}
            cnt = {"na": 0, "ny": 0}

            def a_part(n):
                f, tb = units[n]
                sl = f % NW
                aset = []
                for c in range(4):
                    na = cnt["na"]
                    cnt["na"] += 1
                    bk, r2, a8 = na % 2, na % 2, na % 8
                    for k in range(8):
                        mm(S, psb(bk), w1s[:, sl, k, c * 128:(c + 1) * 128], hT[:, k, tb * 512:(tb + 1) * 512],
                           k == 0, k == 7, [w1s_b[sl], hT_b[tb]], [bank[bk]], k == 7)
                    act(S, rr[:, r2, :], psb(bk), AF.Relu, [bank[bk]], [rr_b[r2]])
                    tt(S, "pool", aT[:, a8, :], rr[:, r2, :], rr[:, r2, :], ALU.mult, [rr_b[r2]], [aT_b[a8]])
                    aset.append(a8)
                asets[n] = aset

            def y_part(n):
                f, tb = units[n]
                sl = f % NW
                aset = asets.pop(n)
                for j in range(8):
                    bk = 2 + (cnt["ny"] % 3)
                    cnt["ny"] += 1
                    for c in range(4):
                        mm(S, psb(bk), w2s[:, sl, c, j * 128:(j + 1) * 128], aT[:, aset[c], :], c == 0, c == 3,
                           [w2s_b[sl], aT_b[aset[c]]], [bank[bk]], c == 3)
                    xv = xT[:, j, tb * 512:(tb + 1) * 512]
                    stt(S, xv, psb(bk), mod_s(l, 5, j), xv, ALU.mult, ALU.add, [bank[bk], modT[l][1], xTb[tb]], [xTb[tb]])
                if last_use[f] == n and f + 3 < 8:
                    load_w(f + 3)

            bg = None
            if l == 0:
                ada_slots2 = [A.tile(f"ada2_{i}", [128, 8, 512], BF16) for i in range(2)]
                bg = mods_gen(1, ada_slots2, 512, [5], 0, 6144, "m", prime=True)
                next(bg, None)
            a_part(0)
            for n in range(len(units)):
                for fn in hooks.get(n, []):
                    fn()
                if n + 1 < len(units):
                    a_part(n + 1)
                y_part(n)
                if bg is not None and n >= 4:
                    next(bg, None)
            if bg is not None:
                for _ in bg:
                    pass

        if stage >= 3:
            mlp(0)

        if stage >= 4:
            gm_make("gm_mix1", V_MIXG1, 1, 1)
            gm_make("gm_mlp1", V_MLPG1, 1, 4)
            A.reset()
            hT1, hT1_b = A.tile("hT1", [128, 8, 1024], BF16, nbufs=2)
            vraw, vraw_b = A.tile("vraw", [128, 8, 3072], BF16, nbufs=8)
            sq_t = A.tile("sq", [128, 8, 512], BF16)
            rs_t = A.tile("rs", [128, 512], F32)
            tmp_t = A.tile("tmp", [128, 2, 512], F32, nbufs=2)
            wins, wins_b = A.tile("wins", [128, 2, 8, 512], BF16, nbufs=2)
            bvs, bvs_b = A.tile("bvs", [1, 2, 512], BF16, nbufs=2)
            wouts, wouts_b = A.tile("wouts", [128, 2, 3, 1024], BF16, nbufs=2)
            wsTb, wsTb_b = A.tile("wsTb", [128, 8, 128], BF16)
            wsr, wsr_b = A.tile("wsr", [128, 2, 8, 128], BF16, nbufs=2)
            bs_bc, bs_b = A.tile("bs_bc", [128, 8, 128], F32)
            uT, uT_b = A.tile("uT", [128, 6, 512], BF16, nbufs=6)
            mT, mT_b = A.tile("mT", [128, 6, 512], BF16, nbufs=6)
            svt, svt_b = A.tile("svt", [128, 2, 512], F32, nbufs=2)
            ssv, ssv_b = A.tile("ssv", [128, 64], F32)
            rs_t2s = A.tile("rs2s", [128, 512], F32)
            S.dma("pool", wsTb, wsT_d, writes=[wsTb_b])
            S.dma("sp", bs_bc.rearrange("p g q -> p (g q)"), bs_d.partition_broadcast(128), writes=[bs_b])
            nwin = 0
            nwo = 0
            nu = 0
            nsv = 0
            for TB in range(2):
                T0 = TB * 1024
                rss1 = [rs_t, rs_t2s]
                for hb in range(2):
                    fm_stat(T0 + hb * 512, 512, sq_t, rss1[hb], [xTb[TB * 2 + hb]])

                def sgu_mod(hb):
                    fm_mod(T0 + hb * 512, 512, gms["gm_mix1"][0], 1, 0, hT1[:, :, hb * 512:(hb + 1) * 512], hT1_b[hb],
                           rss1[hb], tmp_t, [xTb[TB * 2 + hb]], gms["gm_mix1"][1])

                sgu_mod(0)
                def load_v(vb):
                    nonlocal nwin
                    sl = nwin % 2
                    nwin += 1
                    c0 = 3072 + vb * 512
                    S.dma("pool", wins[:, sl], win_d[:, c0:c0 + 512].rearrange("(k p) n -> p k n", p=128),
                          writes=[wins_b[sl]], semkey=f"wins{sl}")
                    S.dma("pool", bvs[:, sl, :], binv_d[:, vb * 512:(vb + 1) * 512], writes=[bvs_b[sl]], semkey=f"bvs{sl}")
                    return sl

                gst = {}
                ust = {}

                def load_g_dma(g):
                    nonlocal nwin, nwo
                    sl = nwo % 2
                    nwo += 1
                    wsl = nwin % 2
                    nwin += 1
                    S.dma("pool", wins[:, wsl, :, 0:384], win_d[:, g * 384:(g + 1) * 384].rearrange("(k p) n -> p k n", p=128),
                          writes=[wins_b[wsl]], semkey=f"wins{wsl}")
                    S.dma("pool", wouts[:, sl], wout_d[g * 384:(g + 1) * 384, :].rearrange("(c p) n -> p c n", p=128),
                          writes=[wouts_b[sl]], semkey=f"wouts{sl}")
                    gst[g] = (sl, wsl)

                vsl = {0: load_v(0)}
                for vb in range(6):
                    if vb + 1 < 6:
                        vsl[vb + 1] = load_v(vb + 1)
                    else:
                        load_g_dma(0)
                    sl = vsl[vb]
                    for i in range(8):
                        if vb == 0 and i == 4:
                            sgu_mod(1)
                        bk = i % 2
                        for k in range(8):
                            mm(S, psb(bk), hT1[:, k, i * 128:(i + 1) * 128], wins[:, sl, k, :], k == 0, False,
                               [hT1_b[i // 4], wins_b[sl]], [bank[bk]], False)
                        mm(S, psb(bk), onesb[0:1, :], bvs[0:1, sl, :], False, True, [onesb_b, bvs_b[sl]], [bank[bk]], True)
                        act(S, vraw[:, i, vb * 512:(vb + 1) * 512], psb(bk), AF.Gelu, [bank[bk]], [vraw_b[i]])
                        act(S, sq_t[0][:, 0, :], vraw[:, i, vb * 512:(vb + 1) * 512], AF.Square, [vraw_b[i]], [sq_t[1], ssv_b],
                            accum_out=ssv[:, i * 6 + vb:i * 6 + vb + 1])
                S.op("dve", lambda e: e.tensor_reduce(out=ssv[:, 48:56], in_=ssv[:, 0:48].rearrange("p (i v) -> p i v", v=6),
                                                      axis=AX.X, op=ALU.add), [ssv_b], [ssv_b])
                act(S, ssv[:, 56:64], ssv[:, 48:56], AF.Sqrt, [ssv_b], [ssv_b], bias=EPS, scale=1.0 / 3072)
                recip(S, ssv[:, 56:64], ssv[:, 56:64], [ssv_b], [ssv_b])
                rv = ssv[:, 56:64]
                its = [(g, half) for g in range(8) for half in range(2)]

                def load_g(g):
                    if g not in gst:
                        load_g_dma(g)
                    sl, wsl = gst[g]
                    tt(S, "pool", wsr[:, sl], wsTb[:, g, :].unsqueeze(1).broadcast_to([128, 8, 128]),
                       rv.unsqueeze(2).broadcast_to([128, 8, 128]), ALU.mult, [wsTb_b, ssv_b], [wsr_b[sl]])

                def u_part(n):
                    nonlocal nu
                    g, half = its[n]
                    sl, wsl = gst[g]
                    us = []
                    for c in range(3):
                        fc = g * 3 + c
                        bk = c % 2
                        u6 = nu % 6
                        nu += 1
                        for k in range(8):
                            mm(S, psb(bk), wins[:, wsl, k, c * 128:(c + 1) * 128], hT1[:, k, half * 512:(half + 1) * 512],
                               k == 0, k == 7, [wins_b[wsl], hT1_b[half]], [bank[bk]], k == 7)
                        act(S, uT[:, u6, :], psb(bk), AF.Gelu, [bank[bk], vecs_b], [uT_b[u6]],
                            bias=vecs[:, V_BINU + fc:V_BINU + fc + 1])
                        us.append(u6)
                    ust[n] = us

                def sv_part(n):
                    nonlocal nsv
                    g, half = its[n]
                    sl, wsl = gst[g]
                    us = ust[n]
                    for c in range(3):
                        fc = g * 3 + c
                        bk = 2 + (nsv % 2)
                        s2 = nsv % 2
                        nsv += 1
                        for i4 in range(4):
                            i = half * 4 + i4
                            mm(S, ps[:, bk, i4 * 128:(i4 + 1) * 128], vraw[:, i, fc * 128:(fc + 1) * 128], wsr[:, sl, i, :],
                               True, True, [vraw_b[i], wsr_b[sl]], [bank[bk]], i4 == 3)
                        stt(S, svt[:, s2, :].rearrange("p (a b) -> p a b", b=128), ps[:, bk, :].rearrange("p (a b) -> p a b", b=128),
                            vecs[:, V_VG + fc:V_VG + fc + 1], bs_bc[:, g, :].unsqueeze(1).broadcast_to([128, 4, 128]),
                            ALU.mult, ALU.add, [bank[bk], vecs_b, bs_b], [svt_b[s2]])
                        tt(S, "pool", mT[:, us[c], :], svt[:, s2, :], uT[:, us[c], :], ALU.mult, [svt_b[s2], uT_b[us[c]]],
                           [mT_b[us[c]]])

                def y_part(n):
                    g, half = its[n]
                    sl, wsl = gst[g]
                    us = ust.pop(n)
                    tb = TB * 2 + half
                    for j in range(8):
                        bk = 4 + (j % 3)
                        for c in range(3):
                            mm(S, psb(bk), wouts[:, sl, c, j * 128:(j + 1) * 128], mT[:, us[c], :], c == 0, c == 2,
                               [wouts_b[sl], mT_b[us[c]]], [bank[bk]], c == 2)
                        xv = xT[:, j, tb * 512:(tb + 1) * 512]
                        stt(S, xv, psb(bk), mod_s(1, 2, j), xv, ALU.mult, ALU.add, [bank[bk], modT[1][1], xTb[tb]], [xTb[tb]])

                load_g(0)
                u_part(0)
                sv_part(0)
                for n in range(len(its)):
                    if its[n][1] == 0 and its[n][0] + 1 < 8:
                        load_g(its[n][0] + 1)
                    if n + 1 < len(its):
                        u_part(n + 1)
                        sv_part(n + 1)
                    y_part(n)

        if stage >= 5:
            mlp(1)

        A.reset()
        fg_bc, fg_b = A.tile("fg_bc", [128, D], F32)
        S.dma("sp", fg_bc, finalg_d.partition_broadcast(128), writes=[fg_b])
        ost, ost_b = A.tile("ost", [128, 2, D], F32, nbufs=2)
        junkf, junkf_b = A.tile("junkf", [128, D], BF16)
        fs, fs_b = A.tile("fs", [128, 4], F32)
        outbufs = [Buf(f"out{i}") for i in range(2)]
        for t in range(16):
            o = t % 2
            for j in range(8):
                tr(S, ps[:, (o * 2) + j // 4, (j % 4) * 128:(j % 4 + 1) * 128], xT[:, j, t * 128:(t + 1) * 128], identf,
                   [xTb[t // 4], identf_b], [bank[o * 2 + j // 4]], j % 4 == 3)
            src2 = ps[:, o * 2:o * 2 + 2, :].rearrange("p a b -> p (a b)")
            bks = [bank[o * 2], bank[o * 2 + 1]]
            if stage >= 6:
                act(S, junkf, src2, AF.Square, bks, [junkf_b, fs_b], scale=1.0 / 32, accum_out=fs[:, 0:1])
                act(S, fs[:, 1:2], fs[:, 0:1], AF.Sqrt, [fs_b], [fs_b], bias=EPS, scale=1.0)
                recip(S, fs[:, 2:3], fs[:, 1:2], [fs_b], [fs_b])
                stt(S, ost[:, o, :], src2, fs[:, 2:3], fg_bc, ALU.mult, ALU.mult, bks + [fs_b, fg_b], [ost_b[o]])
            else:
                cp(S, "dve", ost[:, o, :], src2, bks, [ost_b[o]])
            S.dma("sp", out_d[t * 128:(t + 1) * 128, :], ost[:, o, :], reads=[ost_b[o]], writes=[outbufs[o]], semkey=f"out{o}")
        S.wait_all("sp", outbufs)
        S.emit(block)
    return nc


def _rope_tables():
    n = SEQ
    rows = np.repeat(np.arange(n // 64, dtype=np.int32), 64).astype(np.float32)
    cols = np.tile(np.arange(64, dtype=np.int32), n // 64).astype(np.float32)
    freqs = (1.0 / (np.float32(10000.0) ** (np.arange(0, 64, 2, dtype=np.float32) / np.float32(64)))).astype(np.float32)
    ang = np.concatenate([rows[:, None] * freqs, cols[:, None] * freqs], axis=-1).astype(np.float32)
    return np.cos(ang).astype(np.float32), np.sin(ang).astype(np.float32)


_NC_CACHE = {}


def make_in_maps(inp):
    f = lambda a: np.ascontiguousarray(np.asarray(a, dtype=np.float32))
    cos, sin = _rope_tables()

    def fm(v):
        v = np.asarray(v, dtype=np.float32)
        return v.reshape(-1, 128).T

    vecs = np.concatenate([
        fm(inp["ada_b"][0]), fm(inp["ada_b"][1]), fm(inp["mix_norm_g"][0]), fm(inp["mix_norm_g"][1]),
        fm(inp["mlp_norm_g"][0]), fm(inp["mlp_norm_g"][1]), fm(inp["sgu_b_in"][0][:3072]), fm(inp["sgu_v_g"][0])], axis=1)
    assert vecs.shape == (128, NVEC)
    shared = {
        "ada_w": f(inp["ada_w"]), "vecs": f(vecs), "mixg0": f(inp["mix_norm_g"][0]), "final_g": f(inp["final_g"]),
        "q_g": f(inp["attn_q_g"][0]), "k_g": f(inp["attn_k_g"][0]), "binv": f(inp["sgu_b_in"][0][3072:].reshape(1, 3072)),
        "b_s": f(np.asarray(inp["sgu_b_s"][0]).reshape(1024)), "wqkv": f(inp["attn_wqkv"][0]), "wo": f(inp["attn_wo"][0]),
        "w1": f(inp["mlp_w1"]), "w2": f(inp["mlp_w2"]), "w_in": f(inp["sgu_w_in"][0]), "w_out": f(inp["sgu_w_out"][0]),
        "wsT": f(np.transpose(np.asarray(inp["sgu_w_s"][0]), (2, 0, 1))), "ident": np.eye(128, dtype=np.float32),
    }
    maps = []
    ones_c = np.ones((CTX, 64), np.float32)
    zeros_c = np.zeros((CTX, 64), np.float32)
    for c in range(8):
        b, r = c // 4, c % 4
        x = np.asarray(inp["x"][b], dtype=np.float32)
        m = dict(shared)
        m["xs"] = f(np.roll(x, -r * T, axis=0))
        m["ctx"] = f(inp["ctx"][b])
        m["cs"] = f(np.concatenate([np.roll(cos, -r * T, axis=0), ones_c], axis=0))
        m["sn"] = f(np.concatenate([np.roll(sin, -r * T, axis=0), zeros_c], axis=0))
        m["cvec"] = f(np.stack([np.asarray(inp["c"][b]), np.asarray(inp["c_ctx"])]))
        maps.append(m)
    return maps


def kernel(**inputs):
    if "nc" not in _NC_CACHE:
        _NC_CACHE["nc"] = build()
    nc = _NC_CACHE["nc"]
    maps = make_in_maps(inputs)
    res = run_bass_kernel_spmd(nc, maps, core_ids=list(range(8)))
    out = np.empty((2, SEQ, D), np.float32)
    for c in range(8):
        b, r = c // 4, c % 4
        out[b, r * T:(r + 1) * T] = res.results[c]["out"]
    return out
```
